# Optimizing a Trainium2 kernel written in Bass

```python
import math
import jax, jax.numpy as jnp
from jax import lax
import numpy as np

D_MODEL = 1024
BATCH = 4
SEQ = 8192
DEPTH = 2
DEC_BATCH = 32
DEC_SEQ = 4
PAST_LEN = 16384
PAGE_SIZE = 128

N_AB = (DEPTH + 1) // 2
N_ML = DEPTH // 2

A_HEADS = 8
HEAD_DIM = 64
A_WIDTH = A_HEADS * HEAD_DIM
A_BRANCHES = ((128, 1), (512, 4), (2048, 16))
A_WIN = 2048
A_STEPS = A_BRANCHES[0][0] // A_BRANCHES[0][1]
A_BLK = 128
ATTN_SCALE = HEAD_DIM ** -0.5

POOL_SIZES = (2, 4, 8, 16)
N_POOL = len(POOL_SIZES)
B_WIDTH = D_MODEL // 2
B_GROUP = B_WIDTH // N_POOL
POOL_BUF = max(POOL_SIZES) - 1

AB_IN = 3 * A_WIDTH + B_WIDTH
AB_MIX = A_WIDTH + B_WIDTH

ML_INNER = 2 * D_MODEL
ML_HEADS = 4
ML_HD = ML_INNER // ML_HEADS
ML_CONV = 4
ML_QKV_BLOCK = 4
ML_CHUNK = 64

D_FF = 4 * D_MODEL

RMS_EPS = 1e-6
LN_EPS = 1e-5

kernel_name = 'hybrid_dilated_pool_mlstm_step'


def rms_norm(x, g):
    xf = x.astype(jnp.float32)
    y = xf * lax.rsqrt(jnp.mean(xf * xf, axis=-1, keepdims=True) + RMS_EPS)
    return (y * g.astype(jnp.float32)).astype(x.dtype)


def sq_relu_ffn(x, w1, w2):
    h = jax.nn.relu(x @ w1)
    return (h * h) @ w2


def softmax_stats(s):
    m = jnp.max(s, axis=-1, keepdims=True)
    p = jnp.exp(s - m)
    den = jnp.sum(p, axis=-1, keepdims=True)
    return p / den, (m + jnp.log(den))[..., 0]


def dilated_branch_prompt(q, k, v, dil):
    B, S, H, hd = q.shape
    L = S // dil
    nb = -(-L // A_BLK)
    Lp = nb * A_BLK

    def to_blocks(t):
        t = t.reshape(B, L, dil, H, hd).transpose(0, 2, 1, 3, 4)
        t = jnp.pad(t, ((0, 0), (0, 0), (0, Lp - L), (0, 0), (0, 0)))
        return t.reshape(B, dil, nb, A_BLK, H, hd)

    def band(t):
        prev = jnp.pad(t[:, :, :-1], ((0, 0), (0, 0), (1, 0), (0, 0), (0, 0), (0, 0)))
        return jnp.concatenate([prev, t], axis=3)

    qb = to_blocks(q)
    kk = band(to_blocks(k))
    vv = band(to_blocks(v))
    s = jnp.einsum('brnqhd,brnkhd->brnhqk', qb, kk) * ATTN_SCALE
    qi = jnp.arange(A_BLK)[:, None]
    ki = jnp.arange(2 * A_BLK)[None, :]
    dist = qi - ki + A_BLK
    key_step = jnp.arange(nb)[:, None, None] * A_BLK + ki[None] - A_BLK
    mask = ((dist >= 0) & (dist <= A_STEPS))[None] & (key_step >= 0)
    s = jnp.where(mask[:, None], s, -jnp.inf)
    p, lse = softmax_stats(s)
    o = jnp.einsum('brnhqk,brnkhd->brnqhd', p, vv)
    o = o.reshape(B, dil, Lp, H, hd)[:, :, :L].transpose(0, 2, 1, 3, 4).reshape(B, S, H, hd)
    lse = lse.transpose(0, 1, 2, 4, 3).reshape(B, dil, Lp, H)[:, :, :L].transpose(0, 2, 1, 3).reshape(B, S, H)
    return o, lse


def dilated_branch_sample(q, k_all, v_all, dil, buf_len):
    N, T, H, hd = q.shape
    nk = A_STEPS + 1
    idx = buf_len + jnp.arange(T)[:, None] - dil * jnp.arange(nk)[None, :]
    valid = idx >= 0
    flat = jnp.maximum(idx, 0).reshape(-1)
    kg = jnp.take(k_all, flat, axis=1).reshape(N, T, nk, H, hd)
    vg = jnp.take(v_all, flat, axis=1).reshape(N, T, nk, H, hd)
    s = jnp.einsum('nthd,ntkhd->nthk', q, kg) * ATTN_SCALE
    s = jnp.where(valid[None, :, None, :], s, -jnp.inf)
    p, lse = softmax_stats(s)
    o = jnp.einsum('nthk,ntkhd->nthd', p, vg)
    return o, lse


def combine_branches(outs, lses):
    w = jax.nn.softmax(jnp.stack(lses, 0), axis=0)
    return jnp.einsum('gnth,gnthd->nthd', w, jnp.stack(outs, 0))


def pool_mix(u, buf, start, w_pool, scale):
    N, T, _ = u.shape
    ext = jnp.concatenate([buf.astype(u.dtype), u], axis=1)
    cs = jnp.pad(jnp.cumsum(ext.astype(jnp.float32), axis=1), ((0, 0), (1, 0), (0, 0)))
    pos = (start + jnp.arange(T)).astype(jnp.float32)
    hi = cs[:, POOL_BUF + 1:POOL_BUF + 1 + T]
    means = []
    for g, w in enumerate(POOL_SIZES):
        sl = slice(g * B_GROUP, (g + 1) * B_GROUP)
        lo = cs[:, POOL_BUF + 1 - w:POOL_BUF + 1 - w + T, sl]
        cnt = jnp.minimum(pos + 1.0, float(w))
        means.append((hi[..., sl] - lo) / cnt[None, :, None])
    pooled = jnp.concatenate(means, axis=-1) - u.astype(jnp.float32)
    y = jnp.einsum('ntgc,gcd->ntgd', pooled.reshape(N, T, N_POOL, B_GROUP),
                   w_pool.astype(jnp.float32)).reshape(N, T, B_WIDTH)
    return (y * scale.astype(jnp.float32)).astype(u.dtype), ext[:, T:]


def ab_mix(xn, w_in, w_pool, pool_scale, w_out, kv_buf, pool_buf, start):
    N, T, _ = xn.shape
    proj = xn @ w_in
    q, k, v, u = jnp.split(proj, [A_WIDTH, 2 * A_WIDTH, 3 * A_WIDTH], axis=-1)
    heads = lambda t: t.reshape(N, T, A_HEADS, HEAD_DIM)
    k, v = heads(k), heads(v)
    qf, kf, vf = heads(q).astype(jnp.float32), k.astype(jnp.float32), v.astype(jnp.float32)
    if kv_buf is None:
        res = [dilated_branch_prompt(qf, kf, vf, d) for _, d in A_BRANCHES]
    else:
        k_buf, v_buf = kv_buf
        buf_len = k_buf.shape[1]
        k_all = jnp.concatenate([k_buf.astype(jnp.float32), kf], axis=1)
        v_all = jnp.concatenate([v_buf.astype(jnp.float32), vf], axis=1)
        res = [dilated_branch_sample(qf, k_all, v_all, d, buf_len) for _, d in A_BRANCHES]
    a = combine_branches([r[0] for r in res], [r[1] for r in res]).reshape(N, T, A_WIDTH).astype(xn.dtype)
    b, new_pool = pool_mix(u, pool_buf, start, w_pool, pool_scale)
    y = jnp.concatenate([a, b], axis=-1) @ w_out
    return y, k, v, new_pool


def headwise(x, w):
    N, T, C = x.shape
    xb = x.reshape(N, T, C // ML_QKV_BLOCK, ML_QKV_BLOCK)
    return jnp.einsum('ntgi,gio->ntgo', xb, w).reshape(N, T, C)


def mlstm_cell(q, k, v, i_pre, f_pre, C0, n0, m0):
    N, T, H, d = q.shape
    L = math.gcd(T, ML_CHUNK)
    nc = T // L

    def chunks(t):
        t = t.reshape((N, nc, L) + t.shape[2:])
        return jnp.moveaxis(t, (1, 3), (0, 2))

    xs = (chunks(q), chunks(k), chunks(v), chunks(i_pre), chunks(jax.nn.log_sigmoid(f_pre)))
    causal = jnp.tril(jnp.ones((L, L), dtype=bool))

    def step(carry, blk):
        C, n, m = carry
        qc, kc, vc, ic, fc = blk
        b = jnp.cumsum(fc, axis=-1)
        logw = jnp.where(causal, b[..., :, None] - b[..., None, :] + ic[..., None, :], -jnp.inf)
        inter = b + m[..., None]
        mt = jnp.maximum(inter, jnp.max(logw, axis=-1))
        a = jnp.exp(logw - mt[..., None]) * jnp.einsum('nhsd,nhrd->nhsr', qc, kc)
        si = jnp.exp(inter - mt)
        num = si[..., None] * jnp.einsum('nhsd,nhde->nhse', qc, C) + jnp.einsum('nhsr,nhre->nhse', a, vc)
        den = si * jnp.einsum('nhsd,nhd->nhs', qc, n) + jnp.sum(a, axis=-1)
        hc = num / jnp.maximum(jnp.abs(den), jnp.exp(-mt))[..., None]
        b_last = b[..., -1]
        wr = b_last[..., None] - b + ic
        m_new = jnp.maximum(b_last + m, jnp.max(wr, axis=-1))
        wk = jnp.exp(wr - m_new[..., None])[..., None] * kc
        sc = jnp.exp(b_last + m - m_new)
        C_new = sc[..., None, None] * C + jnp.einsum('nhrd,nhre->nhde', wk, vc)
        n_new = sc[..., None] * n + jnp.sum(wk, axis=2)
        return (C_new, n_new, m_new), hc

    (C, n, m), hs = lax.scan(step, (C0, n0, m0), xs)
    h = jnp.moveaxis(hs, (0, 2), (1, 3)).reshape(N, T, H, d)
    return h, C, n, m


def ml_mix(xn, w_in, w_conv, b_conv, w_q, w_k, w_v, w_i, b_i, w_f, b_f, g_norm, skip, w_out,
           conv_buf, C0, n0, m0):
    N, T, _ = xn.shape
    xm, og = jnp.split(xn @ w_in, 2, axis=-1)
    ext = jnp.concatenate([conv_buf.astype(xm.dtype), xm], axis=1)
    conv = sum((ext[:, j:j + T] * w_conv[j] for j in range(ML_CONV)), b_conv)
    ca = jax.nn.silu(conv)
    q = headwise(ca, w_q)
    k = headwise(ca, w_k)
    v = headwise(xm, w_v)
    qkv = jnp.concatenate([q, k, v], axis=-1)
    i_pre = (qkv @ w_i + b_i).astype(jnp.float32)
    f_pre = (qkv @ w_f + b_f).astype(jnp.float32)
    hd = lambda t: t.reshape(N, T, ML_HEADS, ML_HD).astype(jnp.float32)
    h, C, n, m = mlstm_cell(hd(q), hd(k) * (ML_HD ** -0.5), hd(v), i_pre, f_pre,
                            C0.astype(jnp.float32), n0.astype(jnp.float32), m0.astype(jnp.float32))
    mu = jnp.mean(h, axis=-1, keepdims=True)
    var = jnp.mean(jnp.square(h - mu), axis=-1, keepdims=True)
    hn = ((h - mu) * lax.rsqrt(var + LN_EPS)).reshape(N, T, ML_INNER) * g_norm.astype(jnp.float32)
    y = (hn + skip.astype(jnp.float32) * ca.astype(jnp.float32)) * jax.nn.sigmoid(og.astype(jnp.float32))
    return y.astype(xn.dtype) @ w_out, C, n, m, ext[:, T:]


def setup_inputs(seed: int = 0) -> dict:
    key = jax.random.key(seed)
    keys = iter(jax.random.split(key, 40))

    def rnd(shape, scale):
        return scale * jax.random.normal(next(keys), shape, jnp.float32)

    a_buf = min(A_WIN, PAST_LEN)
    qkv_shape = (N_ML, ML_INNER // ML_QKV_BLOCK, ML_QKV_BLOCK, ML_QKV_BLOCK)
    return {
        'x_prompt': rnd((BATCH, SEQ, D_MODEL), 1.0),
        'x_sample': rnd((DEC_BATCH, DEC_SEQ, D_MODEL), 1.0),
        'cache_a_k': rnd((N_AB, DEC_BATCH, a_buf, A_HEADS, HEAD_DIM), 1.0),
        'cache_a_v': rnd((N_AB, DEC_BATCH, a_buf, A_HEADS, HEAD_DIM), 1.0),
        'state_pool': rnd((N_AB, DEC_BATCH, POOL_BUF, B_WIDTH), 1.0),
        'state_ml_C': rnd((N_ML, DEC_BATCH, ML_HEADS, ML_HD, ML_HD), 0.1),
        'state_ml_n': rnd((N_ML, DEC_BATCH, ML_HEADS, ML_HD), 0.1),
        'state_ml_m': rnd((N_ML, DEC_BATCH, ML_HEADS), 0.5),
        'state_ml_conv': rnd((N_ML, DEC_BATCH, ML_CONV - 1, ML_INNER), 1.0),
        'norm_mix': 1.0 + rnd((DEPTH, D_MODEL), 0.05),
        'norm_ffn': 1.0 + rnd((DEPTH, D_MODEL), 0.05),
        'norm_final': 1.0 + rnd((D_MODEL,), 0.05),
        'ab_w_in': rnd((N_AB, D_MODEL, AB_IN), D_MODEL ** -0.5),
        'ab_w_pool': rnd((N_AB, N_POOL, B_GROUP, B_GROUP), B_GROUP ** -0.5),
        'ab_pool_scale': 1.0 + rnd((N_AB, B_WIDTH), 0.05),
        'ab_w_out': rnd((N_AB, AB_MIX, D_MODEL), AB_MIX ** -0.5),
        'ml_w_in': rnd((N_ML, D_MODEL, 2 * ML_INNER), D_MODEL ** -0.5),
        'ml_w_conv': rnd((N_ML, ML_CONV, ML_INNER), ML_CONV ** -0.5),
        'ml_b_conv': rnd((N_ML, ML_INNER), 0.02),
        'ml_w_q': rnd(qkv_shape, ML_QKV_BLOCK ** -0.5),
        'ml_w_k': rnd(qkv_shape, ML_QKV_BLOCK ** -0.5),
        'ml_w_v': rnd(qkv_shape, ML_QKV_BLOCK ** -0.5),
        'ml_w_i': rnd((N_ML, 3 * ML_INNER, ML_HEADS), 0.1 * (3 * ML_INNER) ** -0.5),
        'ml_b_i': rnd((N_ML, ML_HEADS), 0.1),
        'ml_w_f': rnd((N_ML, 3 * ML_INNER, ML_HEADS), 0.1 * (3 * ML_INNER) ** -0.5),
        'ml_b_f': jnp.linspace(3.0, 6.0, ML_HEADS, dtype=jnp.float32)[None] + rnd((N_ML, ML_HEADS), 0.01),
        'ml_norm': 1.0 + rnd((N_ML, ML_INNER), 0.05),
        'ml_skip': 1.0 + rnd((N_ML, ML_INNER), 0.05),
        'ml_w_out': rnd((N_ML, ML_INNER, D_MODEL), ML_INNER ** -0.5),
        'ffn_w1': rnd((DEPTH, D_MODEL, D_FF), D_MODEL ** -0.5),
        'ffn_w2': rnd((DEPTH, D_FF, D_MODEL), D_FF ** -0.5),
    }


def reference(x_prompt, x_sample, cache_a_k, cache_a_v, state_pool, state_ml_C, state_ml_n, state_ml_m,
              state_ml_conv, norm_mix, norm_ffn, norm_final, ab_w_in, ab_w_pool, ab_pool_scale, ab_w_out,
              ml_w_in, ml_w_conv, ml_b_conv, ml_w_q, ml_w_k, ml_w_v, ml_w_i, ml_b_i, ml_w_f, ml_b_f,
              ml_norm, ml_skip, ml_w_out, ffn_w1, ffn_w2):
    B, S, _ = x_prompt.shape
    NS = x_sample.shape[0]
    a_rows = min(A_WIN, S)
    hp, hs = x_prompt, x_sample
    p_ak, p_av, p_pool, p_C, p_n, p_m, p_conv = [], [], [], [], [], [], []
    s_ak, s_av, s_pool, s_C, s_n, s_m, s_conv = [], [], [], [], [], [], []
    for layer in range(DEPTH):
        j = layer // 2
        if layer % 2 == 0:
            w = (ab_w_in[j], ab_w_pool[j], ab_pool_scale[j], ab_w_out[j])
            y, k, v, pool = ab_mix(rms_norm(hp, norm_mix[layer]), *w, None,
                                   jnp.zeros((B, POOL_BUF, B_WIDTH), hp.dtype), 0)
            hp = hp + y
            p_ak.append(k[:, S - a_rows:])
            p_av.append(v[:, S - a_rows:])
            p_pool.append(pool)
            y, k, v, pool = ab_mix(rms_norm(hs, norm_mix[layer]), *w, (cache_a_k[j], cache_a_v[j]),
                                   state_pool[j], PAST_LEN)
            hs = hs + y
            s_ak.append(k)
            s_av.append(v)
            s_pool.append(pool)
        else:
            w = (ml_w_in[j], ml_w_conv[j], ml_b_conv[j], ml_w_q[j], ml_w_k[j], ml_w_v[j], ml_w_i[j], ml_b_i[j],
                 ml_w_f[j], ml_b_f[j], ml_norm[j], ml_skip[j], ml_w_out[j])
            y, C, n, m, conv = ml_mix(rms_norm(hp, norm_mix[layer]), *w,
                                      jnp.zeros((B, ML_CONV - 1, ML_INNER), hp.dtype),
                                      jnp.zeros((B, ML_HEADS, ML_HD, ML_HD), jnp.float32),
                                      jnp.zeros((B, ML_HEADS, ML_HD), jnp.float32),
                                      jnp.zeros((B, ML_HEADS), jnp.float32))
            hp = hp + y
            p_C.append(C)
            p_n.append(n)
            p_m.append(m)
            p_conv.append(conv)
            y, C, n, m, conv = ml_mix(rms_norm(hs, norm_mix[layer]), *w, state_ml_conv[j],
                                      state_ml_C[j], state_ml_n[j], state_ml_m[j])
            hs = hs + y
            s_C.append(C)
            s_n.append(n)
            s_m.append(m)
            s_conv.append(conv)
        hp = hp + sq_relu_ffn(rms_norm(hp, norm_ffn[layer]), ffn_w1[layer], ffn_w2[layer])
        hs = hs + sq_relu_ffn(rms_norm(hs, norm_ffn[layer]), ffn_w1[layer], ffn_w2[layer])
    y_prompt = rms_norm(hp, norm_final)
    y_sample = rms_norm(hs, norm_final)
    return (y_prompt, y_sample,
            jnp.stack(p_ak), jnp.stack(p_av), jnp.stack(p_pool),
            jnp.stack(p_C), jnp.stack(p_n), jnp.stack(p_m), jnp.stack(p_conv),
            jnp.stack(s_ak), jnp.stack(s_av), jnp.stack(s_pool),
            jnp.stack(s_C), jnp.stack(s_n), jnp.stack(s_m), jnp.stack(s_conv))
```

```python
import contextlib
import math
import numpy as np
import concourse.bass as bass
import concourse.mybir as mybir
from concourse.bass_utils import run_bass_kernel_spmd

F32 = mybir.dt.float32
BF16 = mybir.dt.bfloat16
AF = mybir.ActivationFunctionType
ALU = mybir.AluOpType
AX = mybir.AxisListType

D = 1024
SEQ = 8192
NT = SEQ
NTI = NT // 128
NROW = NT + 128
NSEQ = 4
TS = 4
ABUF = 2048
PAKR = min(ABUF, NT)
RT = 24
NDT = 17
RMS_EPS = 1e-6
LN_EPS = 1e-5
NEGBIG = -30000.0
LAST_PHASE = 99


class Sched:
    def __init__(self, nc, st):
        self.nc = nc
        self.st = st
        self.E = {"pe": nc.tensor, "act": nc.scalar, "dve": nc.vector, "pool": nc.gpsimd, "sp": nc.sync}
        self.sem = {}
        self.cnt = {}
        self.nsem = 0
        for e in self.E:
            self.sem[e] = self._newsem("e_" + e)
            self.cnt[e] = 0
        self.seen = {e: {} for e in self.E}
        self.lastw = {}
        self.readers = {}
        self.pending = {e: [] for e in self.E}
        self.dsem = {}
        self.dcnt = {}
        self.free_dsems = []
        self.all_dsems = []
        self.pool_sems = set()

    def _newsem(self, name):
        self.nsem += 1
        return self.st.enter_context(self.nc.semaphore(name + "_%d" % self.nsem))

    def _waits(self, eng, reads, writes):
        need = {}

        def add(t):
            if t is None:
                return
            sem, val, owner = t
            if owner == eng and (eng == "pe" or SAME_ENGINE_ORDERED):
                return
            k = id(sem)
            if k not in need or need[k][1] < val:
                need[k] = (sem, val)

        for b in reads:
            add(self.lastw.get(b))
            kb = b[0] if isinstance(b, tuple) else b
            if isinstance(kb, str) and kb.startswith("ps"):
                for t in self.readers.get(b, ()):
                    if t[2] != eng:
                        add(t)
        for b in writes:
            add(self.lastw.get(b))
            for t in self.readers.get(b, ()):
                add(t)
        for k, (sem, val) in need.items():
            if self.seen[eng].get(k, 0) >= val:
                continue
            self.E[eng].wait_ge(sem, val)
            self.seen[eng][k] = val

    def op(self, eng, fn, r=(), w=(), inc=True, pw=()):
        self._waits(eng, r, tuple(w) + tuple(pw))
        ins = fn(self.E[eng])
        if not inc:
            self.pending[eng].append(tuple(r))
            return ins
        if self.cnt[eng] >= 30000:
            self.sem[eng] = self._newsem("e_" + eng)
            self.cnt[eng] = 0
        self.cnt[eng] += 1
        ins.then_inc(self.sem[eng], 1)
        tok = (self.sem[eng], self.cnt[eng], eng)
        for pr in self.pending[eng]:
            for b in pr:
                self.readers.setdefault(b, []).append(tok)
        self.pending[eng] = []
        for b in w:
            self.lastw[b] = tok
            self.readers[b] = []
        for b in r:
            self.readers.setdefault(b, []).append(tok)
        return ins

    def dma(self, eng, out, in_, r=(), w=(), key=None, **kw):
        self._waits(eng, r, w)
        if eng == "pool":
            s = self._newsem("q")
            self.all_dsems.append(s)
            self.dcnt[id(s)] = 0
            self.pool_sems.add(id(s))
            self.dsem[("poolq", self.nsem)] = s
            key = ("poolq", self.nsem)
        if key not in self.dsem:
            if self.free_dsems:
                s = self.free_dsems.pop()
            else:
                s = self._newsem("d")
                self.all_dsems.append(s)
                self.dcnt[id(s)] = 0
            self.dsem[key] = s
        s = self.dsem[key]
        self.dcnt[id(s)] += 16
        ins = self.E[eng].dma_start(out=out, in_=in_, **kw)
        ins.then_inc(s, 16)
        tok = (s, self.dcnt[id(s)], "dma")
        for b in w:
            self.lastw[b] = tok
            self.readers[b] = []
        for b in r:
            self.readers.setdefault(b, []).append(tok)
        return ins

    def barrier(self, engines=None):
        for e in self.E:
            assert not self.pending[e], "pending unsignalled reads on %s" % e
        for e in self.E:
            for e2 in self.E:
                if self.cnt[e2] == 0:
                    continue
                k = id(self.sem[e2])
                if self.seen[e].get(k, 0) < self.cnt[e2]:
                    self.E[e].wait_ge(self.sem[e2], self.cnt[e2])
                    self.seen[e][k] = self.cnt[e2]
            for s in self.all_dsems:
                v = self.dcnt[id(s)]
                if v and self.seen[e].get(id(s), 0) < v:
                    self.E[e].wait_ge(s, v)
                    self.seen[e][id(s)] = v
        self.lastw = {}
        self.readers = {}
        self.free_dsems = [x for x in self.all_dsems if id(x) not in self.pool_sems]
        self.dsem = {}


class Ctx:
    pass


class _Stop(Exception):
    pass


STOP_AT = -1
SAME_ENGINE_ORDERED = False


def build_program(last_phase=LAST_PHASE, debug=False):
    nc = bass.Bass("TRN2", target_bir_lowering=False)
    K = Ctx()
    K.nc = nc

    def din(name, shape, dt=F32):
        return nc.dram_tensor(name, list(shape), dt, kind="ExternalInput").ap()

    def dout(name, shape, dt=F32):
        return nc.dram_tensor(name, list(shape), dt, kind="ExternalOutput").ap()

    def dscr(name, shape, dt=F32):
        return nc.dram_tensor(name, list(shape), dt, kind="Internal").ap()

    I = Ctx()
    I.xin = din("xin", [NROW, D])
    I.ck = din("ck", [NSEQ, ABUF, 512])
    I.cv = din("cv", [NSEQ, ABUF, 512])
    I.spool = din("spool", [NSEQ, 15, 512])
    I.sC = din("sC", [NSEQ, 4, 512, 512])
    I.sn = din("sn", [NSEQ, 4, 512])
    I.sm = din("sm", [NSEQ, 4])
    I.sconv = din("sconv", [NSEQ, 3, 2048])
    I.norm_mix = din("norm_mix", [2, D])
    I.norm_ffn = din("norm_ffn", [2, D])
    I.norm_final = din("norm_final", [D])
    I.ab_w_in = din("ab_w_in", [D, 2048])
    I.ab_w_pool = din("ab_w_pool", [4, 128, 128])
    I.ab_pool_scale = din("ab_pool_scale", [512])
    I.ab_w_out = din("ab_w_out", [D, D])
    I.ml_w_in = din("ml_w_in", [D, 4096])
    I.ml_w_conv = din("ml_w_conv", [4, 2048])
    I.ml_b_conv = din("ml_b_conv", [2048])
    I.ml_w_q = din("ml_w_q", [512, 4, 4])
    I.ml_w_k = din("ml_w_k", [512, 4, 4])
    I.ml_w_v = din("ml_w_v", [512, 4, 4])
    I.ml_w_i = din("ml_w_i", [6144, 4])
    I.ml_b_i = din("ml_b_i", [4])
    I.ml_w_f = din("ml_w_f", [6144, 4])
    I.ml_b_f = din("ml_b_f", [4])
    I.ml_norm = din("ml_norm", [2048])
    I.ml_skip = din("ml_skip", [2048])
    I.ml_w_out = din("ml_w_out", [2048, D])
    I.ffn_w1 = din("ffn_w1", [2, D, 4096])
    I.ffn_w2 = din("ffn_w2", [2, 4096, D])
    I.c_ident = din("c_ident", [128, 128])
    I.c_mult = din("c_mult", [128, NDT + 1, 2, 128])
    I.c_eh = din("c_eh", [4, 4, 128])
    I.c_neg = din("c_neg", [128, 128])
    I.c_bd = din("c_bd", [128, 32])
    I.c_invcnt = din("c_invcnt", [128, 4, 16])

    O = Ctx()
    O.y = dout("y", [NROW, D])
    O.pak = dout("pak", [PAKR, 512])
    O.pav = dout("pav", [PAKR, 512])
    O.ppool = dout("ppool", [15, 512])
    O.pC = dout("pC", [4, 512, 512])
    O.pn = dout("pn", [4, 512])
    O.pm = dout("pm", [4])
    O.pconv = dout("pconv", [3, 2048])
    O.sak = dout("sak", [NSEQ * TS, 512])
    O.sav = dout("sav", [NSEQ * TS, 512])
    O.spool = dout("spool_o", [NSEQ, 15, 512])
    O.sC = dout("sC_o", [NSEQ, 4, 512, 512])
    O.sn = dout("sn_o", [NSEQ, 4, 512])
    O.sm = dout("sm_o", [NSEQ, 4])
    O.sconv = dout("sconv_o", [NSEQ, 3, 2048])

    hmk = dout if debug else dscr
    H1 = hmk("H1", [NROW, D])
    H2 = hmk("H2", [NROW, D])
    H3 = hmk("H3", [NROW, D])
    SG = dscr("SG", [NTI + 1, 128, 16 * 128], BF16)
    YT = dscr("YT", [NTI + 1, 128, 16 * 128], BF16)

    with contextlib.ExitStack() as gst:
        S = Sched(nc, gst)
        K.S = S
        ps = [gst.enter_context(nc.psum_tensor("ps%d" % i, [128, 512], F32)) for i in range(8)]
        K.ps = ps

        def mm(out, lhsT, rhs, start, stop, r=(), bank=None, first=False, last=False, inc=None, skip=False):
            w = [bank] if (last and bank is not None) else []
            pw = [bank] if (first and bank is not None) else []
            if inc is None:
                inc = bool(w)
            return S.op("pe", lambda e: e.matmul(out, lhsT=lhsT, rhs=rhs, start=start, stop=stop, skip_group_check=skip),
                        r=r, w=w, inc=inc, pw=pw)

        def tr(out, in_, ident, r=(), bank=None, first=False, last=False):
            w = [bank] if (last and bank is not None) else []
            pw = [bank] if (first and bank is not None) else []
            return S.op("pe", lambda e: e.transpose(out, in_, ident), r=r, w=w, inc=bool(w), pw=pw)

        def act(out, in_, func, r=(), w=(), bias=None, scale=None, accum_out=None):
            kw = {}
            if bias is not None:
                kw["bias"] = bias
            if scale is not None:
                kw["scale"] = scale
            if accum_out is not None:
                kw["accum_out"] = accum_out
            return S.op("act", lambda e: e.activation(out=out, in_=in_, func=func, **kw), r=r, w=w)

        def tt(eng, out, in0, in1, op, r=(), w=()):
            return S.op(eng, lambda e: e.tensor_tensor(out=out, in0=in0, in1=in1, op=op), r=r, w=w)

        def tsc(eng, out, in0, s1, op0, s2=None, op1=None, r=(), w=()):
            if op1 is None:
                return S.op(eng, lambda e: e.tensor_scalar(out=out, in0=in0, scalar1=s1, scalar2=None, op0=op0), r=r, w=w)
            return S.op(eng, lambda e: e.tensor_scalar(out=out, in0=in0, scalar1=s1, scalar2=s2, op0=op0, op1=op1), r=r, w=w)

        def stt(out, in0, scalar, in1, op0, op1, r=(), w=()):
            return S.op("dve", lambda e: e.scalar_tensor_tensor(out=out, in0=in0, scalar=scalar, in1=in1, op0=op0, op1=op1), r=r, w=w)

        def cp(eng, out, in_, r=(), w=()):
            if eng == "act":
                return S.op("act", lambda e: e.copy(out=out, in_=in_), r=r, w=w)
            return S.op(eng, lambda e: e.tensor_copy(out=out, in_=in_), r=r, w=w)

        def mset(eng, ap, val, w=()):
            return S.op(eng, lambda e: e.memset(ap, val), r=(), w=w)

        K.mm, K.tr, K.act, K.tt, K.tsc, K.stt, K.cp, K.mset = mm, tr, act, tt, tsc, stt, cp, mset

        def chk(n):
            if STOP_AT == n or (n == 23 and STOP_AT == 231):
                S.barrier()
                K.stopped = True
                return True
            return False

        cst = gst
        ident_f = cst.enter_context(nc.sbuf_tensor("ident_f", [128, 128], F32))
        ident_b = cst.enter_context(nc.sbuf_tensor("ident_b", [128, 128], BF16))
        S.dma("sp", ident_f[:, :], I.c_ident[:, :], w=["ident_f"], key="ident_f")
        S.dma("pool", ident_b[:, :], I.c_ident[:, :], w=["ident_b"], key="ident_b")
        K.ident_f, K.ident_b = ident_f, ident_b
        eps_rms = cst.enter_context(nc.sbuf_tensor("eps_rms", [128, 1], F32))
        eps_ln = cst.enter_context(nc.sbuf_tensor("eps_ln", [128, 1], F32))
        mset("dve", eps_rms[:, :], RMS_EPS, w=["eps_rms"])
        mset("dve", eps_ln[:, :], LN_EPS, w=["eps_ln"])

        def rsqrt_(x, n, bias_tile, scale, key):
            act(x, x, AF.Sqrt, r=[key], w=[key], bias=bias_tile[0:n, 0:1], scale=scale)
            S.op("dve", lambda e: e.reciprocal(out=x, in_=x), r=[key], w=[key])

        def norm_T(xt, gain, xn, ss, xnT_dst, keys, psT, evac="act"):
            kx, kn, ks, kT = keys["xt"], keys["xn"], keys["ss"], keys["xnT"]
            act(xn, xt, AF.Square, r=[kx], w=[kn, ks], accum_out=ss)
            rsqrt_(ss, 128, eps_rms, 1.0 / D, ks)
            stt(xn, xt, ss, gain, ALU.mult, ALU.mult, r=[kx, ks, "gain"], w=[kn])
            pT = psT[:, :].bitcast(BF16)
            for kc in range(8):
                tr(pT[:, kc * 128:(kc + 1) * 128], xn[:, kc * 128:(kc + 1) * 128], ident_b[:, :],
                   r=[kn, "ident_b"], bank="psT", first=(kc == 0), last=(kc == 7))
            cp(evac, xnT_dst, pT[:, 0:1024].rearrange("p (k t) -> p k t", k=8), r=["psT"], w=[kT])

        def phase_A():
            with contextlib.ExitStack() as st:
                def sb(name, shape, dt, stack=st):
                    return stack.enter_context(nc.sbuf_tensor("A_" + name, list(shape), dt))
                Win = sb("Win", [128, 8, 2048], BF16)
                Wout = sb("Wout", [128, 8, 1024], BF16)
                wpool = sb("wpool", [128, 4, 128], BF16)
                pscale = sb("pscale", [128, 4], F32)
                gain = sb("gain", [128, D], F32)
                mult = sb("mult", [128, NDT + 1, 2, 128], BF16)
                invc = sb("invc", [128, 4, 16], F32)
                xt = [sb("xt%d" % i, [128, D], F32) for i in range(4)]
                xn = [sb("xn%d" % i, [128, D], BF16) for i in range(4)]
                ss = [sb("ss%d" % i, [128, 1], F32) for i in range(4)]
                xnT = sb("xnT", [128, 8, 512], BF16)
                QT = sb("QT", [128, 8, 512], BF16)
                sA = sb("sA", [128, 528], F32)
                sB = sb("sB", [128, 528], F32)
                pooled = sb("pooled", [128, 512], BF16)
                PT = [sb("PT%d" % i, [128, 512], BF16) for i in range(10)]
                Osb = sb("Osb", [128, 8, 65], F32)
                rden = sb("rden", [128, 8], F32)
                atm = sb("atm", [128, 8, 64], BF16)
                hb = [sb("hb%d" % i, [128, D], F32) for i in range(4)]
                kvst = [sb("kvst%d" % i, [128, 512], F32) for i in range(2)]

                psT, psP, psS, psO = ps[0], [ps[1], ps[2]], [ps[3], ps[4]], [ps[5], ps[6]]
                cnt = {"P": 0, "S": 0, "PT": 0, "hb": 0, "kv": 0, "xt": 0, "ev": 0}

                def load_consts():
                    S.dma("pool", Win[:, :, :], I.ab_w_in.rearrange("(k p) n -> p k n", p=128), w=["Win"], key="Win")
                    S.dma("pool", Wout[:, :, :], I.ab_w_out.rearrange("(k p) n -> p k n", p=128), w=["Wout"], key="Wout")
                    S.dma("pool", wpool[:, :, :], I.ab_w_pool.rearrange("g c d -> c g d"), w=["wpool"], key="wpool")
                    S.dma("sp", pscale[:, :], I.ab_pool_scale.rearrange("(g p) -> p g", p=128), w=["pscale"], key="pscale",
                          allow_slow_non_contiguous=True)
                    S.dma("sp", gain[:, :], I.norm_mix[0].partition_broadcast(128), w=["gain"], key="gain")
                    S.dma("pool", mult[:, :, :, :], I.c_mult[:, :, :, :], w=["mult"], key="mult")
                    S.dma("sp", invc[:, :, :], I.c_invcnt[:, :, :], w=["invc"], key="invc")
                    mset("pool", QT[:, :, :], 0.0, w=["QT"])
                    mset("dve", sA[:, :], 0.0, w=["sA"])
                    mset("dve", sB[:, :], 0.0, w=["sB"])

                def nextP():
                    cnt["P"] += 1
                    return cnt["P"] % 2

                def evac_eng():
                    cnt["ev"] += 1
                    return "act" if cnt["ev"] % 2 else "dve"

                def pro_norm(rows_ap, t):
                    S.dma("sp", xt[t][:, :], rows_ap, w=[("xt", t)], key=("xt", t))
                    act(xn[t][:, :], xt[t][:, :], AF.Square, r=[("xt", t)], w=[("xn", t), ("ss", t)], accum_out=ss[t][:, :])
                    rsqrt_(ss[t][:, :], 128, eps_rms, 1.0 / D, ("ss", t))
                    stt(xn[t][:, :], xt[t][:, :], ss[t][:, :], gain[:, :], ALU.mult, ALU.mult,
                        r=[("xt", t), ("ss", t), "gain"], w=[("xn", t)])

                def pro_T(t, xnT_dst):
                    pT = psT[:, :].bitcast(BF16)
                    for kc in range(8):
                        tr(pT[:, kc * 128:(kc + 1) * 128], xn[t][:, kc * 128:(kc + 1) * 128], ident_b[:, :],
                           r=[("xn", t), "ident_b"], bank="psT", first=(kc == 0), last=(kc == 7))
                    cp("act", xnT_dst, pT[:, 0:1024].rearrange("p (k t) -> p k t", k=8), r=["psT"], w=["xnT"])

                def load_x(rows_ap, xnT_dst):
                    pro_norm(rows_ap, 0)
                    pro_T(0, xnT_dst)

                def attention(qcols_fn, NQ, klist, kt_ap, v_ap, kkey, vkey, mask_ap, qkey):
                    first, last = klist[0], klist[-1]
                    per = max(1, min(len(klist), 512 // (2 * NQ)))
                    units = [(c, klist[p0:p0 + per]) for c in range(4) for p0 in range(0, len(klist), per)]
                    AHEAD = 8
                    slots = {}

                    def s_stage(ui):
                        c, kts = units[ui]
                        n = len(kts)
                        cnt["S"] += 1
                        sbk = cnt["S"] % 2
                        Sb = psS[sbk]
                        for i, kt in enumerate(kts):
                            for h in range(2):
                                mm(Sb[:, (i * 2 + h) * NQ:(i * 2 + h + 1) * NQ],
                                   kt_ap(kt, c, h), qcols_fn(c, h), True, True,
                                   r=[kkey(kt), qkey], bank=("psS", sbk),
                                   first=(i == 0 and h == 0), last=(i == n - 1 and h == 1))
                        cnt["PT"] += 1
                        pk = cnt["PT"] % len(PT)
                        slots[ui] = pk
                        P = PT[pk]
                        act(P[:, 0:n * 2 * NQ], Sb[:, 0:n * 2 * NQ], AF.Exp, r=[("psS", sbk)], w=[("PT", pk)])
                        me = "pool" if cnt["PT"] % 3 == 0 else "dve"
                        Pv = P[:, 0:n * 2 * NQ].rearrange("p (a h q) -> p a h q", a=n, h=2)
                        tt(me, Pv, Pv, mask_ap(kts), ALU.mult, r=[("PT", pk), "mult"], w=[("PT", pk)])

                    def pv_stage(ui):
                        c, kts = units[ui]
                        n = len(kts)
                        pk = slots[ui]
                        P = PT[pk]
                        for i, kt in enumerate(kts):
                            for h in range(2):
                                head = 2 * c + h
                                ob = head // 4
                                fin = (kt == last and h == 1 and c in (1, 3))
                                lastpv = (i == n - 1 and h == 1)
                                mm(psO[ob][0:NQ, (head % 4) * 65:(head % 4) * 65 + 65],
                                   P[:, (i * 2 + h) * NQ:(i * 2 + h + 1) * NQ], v_ap(kt, head),
                                   (kt == first and head % 4 == 0), kt == last,
                                   r=[("PT", pk), vkey(kt)], bank=("psO", ob), first=(kt == first), last=fin,
                                   inc=(fin or lastpv), skip=True)

                    for ui in range(min(AHEAD, len(units))):
                        s_stage(ui)
                    for ui in range(len(units)):
                        pv_stage(ui)
                        if ui + AHEAD < len(units):
                            s_stage(ui + AHEAD)
                    return False

                def finish_attention(NQ, at_dst, at_key):
                    cp("act", Osb[0:NQ, 0:4, :], psO[0][0:NQ, 0:260].rearrange("p (h e) -> p h e", h=4),
                       r=[("psO", 0)], w=["Osb0"])
                    cp("dve", Osb[0:NQ, 4:8, :], psO[1][0:NQ, 0:260].rearrange("p (h e) -> p h e", h=4),
                       r=[("psO", 1)], w=["Osb1"])
                    S.op("dve", lambda e: e.reciprocal(out=rden[0:NQ, :], in_=Osb[0:NQ, :, 64]),
                         r=["Osb0", "Osb1"], w=["rden"])
                    tt("dve", atm[0:NQ, :, :], Osb[0:NQ, :, 0:64],
                       rden[0:NQ, :].unsqueeze(2).broadcast_to([NQ, 8, 64]), ALU.mult,
                       r=["Osb0", "Osb1", "rden"], w=["atm"])
                    pT = psT[:, :].bitcast(BF16)
                    af = atm[0:NQ, :, :].rearrange("p h e -> p (h e)")
                    for c in range(4):
                        tr(pT[:, c * NQ:(c + 1) * NQ], af[:, c * 128:(c + 1) * 128], ident_b[0:NQ, 0:NQ],
                           r=["atm", "ident_b"], bank="psT", first=(c == 0), last=(c == 3))
                    cp("act", at_dst, pT[:, 0:4 * NQ].rearrange("p (c q) -> p c q", c=4), r=["psT"], w=[at_key])

                def pool_mix(uT_ext, ntok, first, bt_dst, bt_key, ukey):
                    W = 16 + ntok
                    for g, wdw in enumerate((2, 4, 8, 16)):
                        u = uT_ext[:, g, :]
                        cur = u
                        curkey = ukey
                        sh = 1
                        k = 0
                        while sh < wdw:
                            dst = sA if k % 2 == 0 else sB
                            dkey = "sA" if k % 2 == 0 else "sB"
                            tt("dve", dst[:, sh:W], cur[:, sh:W], cur[:, 0:W - sh], ALU.add, r=[curkey], w=[dkey])
                            cur, curkey = dst, dkey
                            sh *= 2
                            k += 1
                        stt(pooled[:, 0:ntok], cur[:, 16:W], 1.0 / wdw, u[:, 16:W], ALU.mult, ALU.subtract,
                            r=[ukey, curkey], w=["pooled"])
                        if first:
                            other = sB if curkey == "sA" else sA
                            okey = "sB" if curkey == "sA" else "sA"
                            tt("dve", other[:, 0:16], cur[:, 16:32], invc[:, g, :], ALU.mult, r=[curkey, "invc"], w=[okey])
                            tt("dve", pooled[:, 0:16], other[:, 0:16], u[:, 16:32], ALU.subtract, r=[okey, ukey], w=["pooled"])
                        pb = nextP()
                        mm(psP[pb][:, 0:ntok], wpool[:, g, :], pooled[:, 0:ntok], True, True,
                           r=["wpool", "pooled"], bank=("psP", pb), first=True, last=True)
                        tsc("dve", bt_dst[:, g, :], psP[pb][:, 0:ntok], pscale[:, g:g + 1], ALU.mult,
                            r=[("psP", pb), "pscale"], w=[bt_key])

                def out_proj(at_fn, bt_fn, rows_src, rows_dst, rkeys, hk=None):
                    if hk is None:
                        hk = 0
                        S.dma("sp", hb[hk][:, :], rows_src, w=[("hb", hk)], key=("hb", hk))
                    for half in range(2):
                        pb = nextP()
                        for c in range(8):
                            lhs = at_fn(c) if c < 4 else bt_fn(c - 4)
                            mm(psP[pb][:, :], lhs, Wout[:, c, half * 512:(half + 1) * 512], c == 0, c == 7,
                               r=rkeys + ["Wout"], bank=("psP", pb), first=(c == 0), last=(c == 7))
                        tt("dve", hb[hk][:, half * 512:(half + 1) * 512], psP[pb][:, :],
                           hb[hk][:, half * 512:(half + 1) * 512], ALU.add,
                           r=[("psP", pb), ("hb", hk)], w=[("hb", hk)])
                    S.dma("sp", rows_dst, hb[hk][:, :], r=[("hb", hk)], w=[], key=("hb", hk))

                def proj_tm(xcols_fn, col0, nrow):
                    pb = nextP()
                    for kc in range(8):
                        mm(psP[pb][0:nrow, :], xcols_fn(kc), Win[:, kc, col0:col0 + 512], kc == 0, kc == 7,
                           r=["xnT", "Win"], bank=("psP", pb), first=(kc == 0), last=(kc == 7))
                    return pb

                def to_out(pb, nrow, dst_ap, row0=0):
                    cnt["kv"] += 1
                    kk = cnt["kv"] % 2
                    cp("dve", kvst[kk][0:128, :], psP[pb][0:128, :], r=[("psP", pb)], w=[("kvst", kk)])
                    if STOP_AT == 231:
                        return
                    S.dma("sp", dst_ap, kvst[kk][row0:row0 + nrow, :], r=[("kvst", kk)], w=[], key=("kvst", kk))

                load_consts()
                if chk(1):
                    return
                with contextlib.ExitStack() as st1:
                    KT = sb("KT", [128, 4, RT * 128], BF16, st1)
                    V = sb("V", [128, RT, 8, 65], BF16, st1)
                    uT = sb("uT", [128, 4, 16 + 512], F32, st1)
                    AT = sb("AT", [128, 4, 512], BF16, st1)
                    BT = sb("BT", [128, 4, 512], BF16, st1)
                    mset("pool", V[:, :, :, :], 1.0, w=[("V", i) for i in range(RT)])
                    mset("dve", uT[:, :, 0:16], 0.0, w=["uT"])
                    for t in range(4):
                        pro_norm(I.xin[t * 128:(t + 1) * 128, :], t)
                    for g in range(NTI // 4):
                        for t in range(4):
                            pro_T(t, xnT[:, :, t * 128:(t + 1) * 128])
                        for t in range(4):
                            ti = 4 * g + t
                            S.dma("sp", hb[t][:, :], I.xin[ti * 128:(ti + 1) * 128, :], w=[("hb", t)], key=("hb", t))
                        if chk(2):
                            return
                        for oc in range(16):
                            if 8 <= oc < 12:
                                continue
                            pb = nextP()
                            for kc in range(8):
                                mm(psP[pb][:, :], Win[:, kc, oc * 128:(oc + 1) * 128], xnT[:, kc, :], kc == 0, kc == 7,
                                   r=["xnT", "Win"], bank=("psP", pb), first=(kc == 0), last=(kc == 7))
                            if oc < 4:
                                act(QT[0:64, 2 * oc, :], psP[pb][0:64, :], AF.Copy, r=[("psP", pb)], w=["QT"], scale=0.125)
                                act(QT[64:128, 2 * oc + 1, :], psP[pb][64:128, :], AF.Copy, r=[("psP", pb)], w=["QT"], scale=0.125)
                            elif oc < 8:
                                sl = (4 * g) % RT
                                cp(evac_eng(), KT[:, oc - 4, sl * 128:(sl + 4) * 128], psP[pb][:, :], r=[("psP", pb)],
                                   w=[("K", (4 * g + i) % RT) for i in range(4)])
                            else:
                                cp(evac_eng(), uT[:, oc - 12, 16:528], psP[pb][:, :], r=[("psP", pb)], w=["uT"])
                        if chk(21):
                            return
                        for t in range(4):
                            ti = 4 * g + t
                            sl = ti % RT
                            want_out = ti >= NTI - PAKR // 128
                            r0 = (ti - (NTI - PAKR // 128)) * 128
                            pb = proj_tm(lambda kc: xnT[:, kc, t * 128:(t + 1) * 128], 1024, 128)
                            cp("act", V[:, sl, :, 0:64], psP[pb][:, :].rearrange("p (h e) -> p h e", h=8),
                               r=[("psP", pb)], w=[("V", sl)])
                            if chk(22):
                                return
                            if want_out:
                                to_out(pb, 128, O.pav[r0:r0 + 128, :])
                                if chk(23):
                                    return
                                pb = proj_tm(lambda kc: xnT[:, kc, t * 128:(t + 1) * 128], 512, 128)
                                to_out(pb, 128, O.pak[r0:r0 + 128, :])
                            if ti == NTI - 1:
                                pb = proj_tm(lambda kc: xnT[:, kc, t * 128:(t + 1) * 128], 1536, 128)
                                to_out(pb, 15, O.ppool[:, :], row0=113)
                        if chk(3):
                            return
                        for t in range(4):
                            qt = 4 * g + t
                            klist = list(range(max(0, qt - 16), qt + 1))
                            if attention(
                                lambda c, h: QT[:, 2 * c + h, t * 128:(t + 1) * 128], 128, klist,
                                lambda kt, c, h: KT[:, c, (kt % RT) * 128:(kt % RT + 1) * 128],
                                lambda kt, head: V[:, kt % RT, head, :],
                                lambda kt: ("K", kt % RT), lambda kt: ("V", kt % RT),
                                lambda kts: mult[:, kts[0] - (qt - 16):kts[0] - (qt - 16) + len(kts), :, :], "QT"):
                                return
                            if chk(35):
                                return
                            finish_attention(128, AT[:, :, t * 128:(t + 1) * 128], "AT")
                            if t == 0 and g + 1 < NTI // 4:
                                for t2 in range(4):
                                    ti2 = 4 * (g + 1) + t2
                                    pro_norm(I.xin[ti2 * 128:(ti2 + 1) * 128, :], t2)
                            if chk(36):
                                return
                        if chk(4):
                            return
                        pool_mix(uT, 512, g == 0, BT, "BT", "uT")
                        if chk(5):
                            return
                        cp("dve", uT[:, :, 0:16], uT[:, :, 512:528], r=["uT"], w=["uT"])
                        for t in range(4):
                            ti = 4 * g + t
                            out_proj(lambda c: AT[:, c, t * 128:(t + 1) * 128], lambda c: BT[:, c, t * 128:(t + 1) * 128],
                                     I.xin[ti * 128:(ti + 1) * 128, :], H1[ti * 128:(ti + 1) * 128, :], ["AT", "BT"], hk=t)
                    S.barrier()
                if chk(6):
                    return
                with contextlib.ExitStack() as st2:
                    uTs = sb("uTs", [128, 4, 128], F32, st2)
                    stg = sb("stg", [128, 16, 512], BF16, st2)
                    KTs = sb("KTs", [128, 4, 17 * 128], BF16, st2)
                    Vs = sb("Vs", [128, 17, 8, 65], BF16, st2)
                    ATs = sb("ATs", [128, 4, 128], BF16, st2)
                    BTs = sb("BTs", [128, 4, 128], BF16, st2)
                    pbuf = sb("pbuf", [15, 512], F32, st2)
                    uxs = sb("uxs", [128, 4, 16 + TS], F32, st2)
                    xnTs = xnT[:, :, 0:128]
                    load_x(I.xin[NT:NT + 128, :], xnTs)
                    QTs = QT[:, :, 0:128]
                    kTn = sb("kTn", [128, 4, 128], BF16, st2)
                    for oc in range(16):
                        if 8 <= oc < 12:
                            continue
                        pb = nextP()
                        for kc in range(8):
                            mm(psP[pb][:, 0:128], Win[:, kc, oc * 128:(oc + 1) * 128], xnTs[:, kc, :], kc == 0, kc == 7,
                               r=["xnT", "Win"], bank=("psP", pb), first=(kc == 0), last=(kc == 7))
                        if oc < 4:
                            act(QTs[0:64, 2 * oc, :], psP[pb][0:64, 0:128], AF.Copy, r=[("psP", pb)], w=["QT"], scale=0.125)
                            act(QTs[64:128, 2 * oc + 1, :], psP[pb][64:128, 0:128], AF.Copy, r=[("psP", pb)], w=["QT"], scale=0.125)
                        elif oc < 8:
                            cp("dve", kTn[:, oc - 4, :], psP[pb][:, 0:128], r=[("psP", pb)], w=["kTn"])
                        else:
                            cp("dve", uTs[:, oc - 12, :], psP[pb][:, 0:128], r=[("psP", pb)], w=["uTs"])
                    pb = proj_tm(lambda kc: xnTs[:, kc, :], 512, 128)
                    to_out(pb, NSEQ * TS, O.sak[:, :])
                    pb = proj_tm(lambda kc: xnTs[:, kc, :], 1024, 128)
                    to_out(pb, NSEQ * TS, O.sav[:, :])
                    mset("pool", Vs[:, :, :, :], 1.0, w=["Vs", "Vs16"])
                    mset("pool", Vs[:, 16, :, 0:64], 0.0, w=["Vs16"])
                    mset("dve", KTs[:, :, 16 * 128:17 * 128], 0.0, w=["KTs16"])
                    mset("dve", ATs[:, :, :], 0.0, w=["ATs"])
                    mset("dve", BTs[:, :, :], 0.0, w=["BTs"])
                    mset("dve", uxs[:, :, :], 0.0, w=["uxs"])
                    pT = psT[:, :].bitcast(BF16)
                    for n in range(NSEQ):
                        c0 = n * TS
                        S.dma("pool", stg[:, :, :], I.ck[n].rearrange("(t p) f -> p t f", p=128), w=["stg"], key="stg")
                        for t2 in range(0, 16, 2):
                            for a in range(2):
                                for c in range(4):
                                    tr(pT[:, (a * 4 + c) * 128:(a * 4 + c + 1) * 128],
                                       stg[:, t2 + a, c * 128:(c + 1) * 128], ident_b[:, :],
                                       r=["stg", "ident_b"], bank="psT", first=(a == 0 and c == 0), last=(a == 1 and c == 3))
                            cp(evac_eng(), KTs[:, :, t2 * 128:(t2 + 2) * 128].rearrange("p c (a k) -> p c a k", a=2),
                               pT[:, 0:1024].rearrange("p (a c k) -> p c a k", a=2, c=4), r=["psT"], w=["KTs"])
                        S.dma("pool", stg[:, :, :], I.cv[n].rearrange("(t p) f -> p t f", p=128), w=["stg"], key="stg")
                        cp("pool", Vs[:, 0:16, :, 0:64], stg[:, :, :].rearrange("p t (h e) -> p t h e", h=8), r=["stg"], w=["Vs"])
                        cp("dve", KTs[:, :, 16 * 128:16 * 128 + TS], kTn[:, :, c0:c0 + TS], r=["kTn"], w=["KTs16"])
                        pb = proj_tm(lambda kc: xnTs[:, kc, c0:c0 + TS], 1024, TS)
                        cp("act", Vs[0:TS, 16, :, 0:64], psP[pb][0:TS, :].rearrange("p (h e) -> p h e", h=8),
                           r=[("psP", pb)], w=["Vs16"])
                        attention(
                            lambda c, h: QTs[:, 2 * c + h, c0:c0 + TS], TS, list(range(17)),
                            lambda kt, c, h: KTs[:, c, kt * 128:(kt + 1) * 128],
                            lambda kt, head: Vs[:, kt, head, :],
                            lambda kt: ("KTs16" if kt == 16 else "KTs"), lambda kt: ("Vs16" if kt == 16 else "Vs"),
                            lambda kts: mult[:, kts[0]:kts[0] + len(kts), :, 0:TS], "QT")
                        finish_attention(TS, ATs[:, :, c0:c0 + TS], "ATs")
                        S.dma("sp", pbuf[:, :], I.spool[n], w=["pbuf"], key="pbuf")
                        pf = ps[7]
                        for gq in range(4):
                            tr(pf[:, gq * 16:gq * 16 + 15], pbuf[:, gq * 128:(gq + 1) * 128], ident_f[0:15, 0:15],
                               r=["pbuf", "ident_f"], bank="ps7", first=(gq == 0), last=(gq == 3))
                        cp("dve", uxs[:, :, 1:16], pf[:, 0:64].rearrange("p (g j) -> p g j", g=4)[:, :, 0:15],
                           r=["ps7"], w=["uxs"])
                        cp("dve", uxs[:, :, 16:16 + TS], uTs[:, :, c0:c0 + TS], r=["uTs"], w=["uxs"])
                        pool_mix(uxs, TS, False, BTs[:, :, c0:c0 + TS], "BTs", "uxs")
                        S.dma("sp", O.spool[n, 0:15 - TS, :], I.spool[n, TS:15, :], w=[], key=("spc", n))
                        pb = proj_tm(lambda kc: xnTs[:, kc, c0:c0 + TS], 1536, TS)
                        to_out(pb, TS, O.spool[n, 15 - TS:15, :])
                    out_proj(lambda c: ATs[:, c, :], lambda c: BTs[:, c, :], I.xin[NT:NT + 128, :], H1[NT:NT + 128, :],
                             ["ATs", "BTs"])
                    S.barrier()

        def phase_FFN(layer, Hin, Hout, final):
            GT = 2
            with contextlib.ExitStack() as st:
                def sb(name, shape, dt):
                    return st.enter_context(nc.sbuf_tensor("F%d_" % layer + name, list(shape), dt))
                W1 = sb("W1", [128, 8, 4096], BF16)
                W2 = sb("W2", [128, 32, 1024], BF16)
                gain = sb("gain", [128, D], F32)
                gfin = sb("gfin", [128, D], F32) if final else None
                xt = [sb("xt%d" % i, [128, D], F32) for i in range(2 * GT)]
                xn = [sb("xn%d" % i, [128, D], BF16) for i in range(GT)]
                ss = [sb("ss%d" % i, [128, 1], F32) for i in range(GT)]
                xnT = [sb("xnT%d" % i, [128, 8, 128 * GT], BF16) for i in range(2)]
                hT = sb("hT", [128, 32, 128 * GT], BF16)
                rl = [sb("rl%d" % i, [128, 128 * GT], BF16) for i in range(2)]
                yo = [sb("yo%d" % i, [128, D], F32) for i in range(2)] if final else None
                ssf = sb("ssf", [128, 1], F32)
                S.dma("pool", W1[:, :, :], I.ffn_w1[layer].rearrange("(k p) n -> p k n", p=128), w=["W1"], key="W1")
                S.dma("pool", W2[:, :, :], I.ffn_w2[layer].rearrange("(f p) n -> p f n", p=128), w=["W2"], key="W2")
                S.dma("sp", gain[:, :], I.norm_ffn[layer].partition_broadcast(128), w=["gain"], key="gain")
                if final:
                    S.dma("sp", gfin[:, :], I.norm_final.partition_broadcast(128), w=["gfin"], key="gfin")
                psT, psP, psY = ps[0], [ps[1], ps[2], ps[3]], [ps[4], ps[5], ps[6], ps[7]]
                cnt = {"P": 0, "Y": 0, "rl": 0, "yo": 0}
                groups = [list(range(GT * g, GT * g + GT)) for g in range(NTI // GT)] + [[NTI]]

                def pro_norm(gi):
                    for t, ti in enumerate(groups[gi]):
                        slot = (gi % 2) * GT + t
                        S.dma("sp", xt[slot][:, :], Hin[ti * 128:(ti + 1) * 128, :], w=[("xt", slot)], key=("xt", slot))
                        act(xn[t][:, :], xt[slot][:, :], AF.Square, r=[("xt", slot)], w=[("xn", t), ("ss", t)], accum_out=ss[t][:, :])
                        rsqrt_(ss[t][:, :], 128, eps_rms, 1.0 / D, ("ss", t))
                        stt(xn[t][:, :], xt[slot][:, :], ss[t][:, :], gain[:, :], ALU.mult, ALU.mult,
                            r=[("xt", slot), ("ss", t), "gain"], w=[("xn", t)])

                def pro_T(gi):
                    xb = xnT[gi % 2]
                    pT = psT[:, :].bitcast(BF16)
                    for t, ti in enumerate(groups[gi]):
                        for kc in range(8):
                            tr(pT[:, kc * 128:(kc + 1) * 128], xn[t][:, kc * 128:(kc + 1) * 128], ident_b[:, :],
                               r=[("xn", t), "ident_b"], bank="psT", first=(kc == 0), last=(kc == 7))
                        cp("act", xb[:, :, t * 128:(t + 1) * 128], pT[:, 0:1024].rearrange("p (k t) -> p k t", k=8),
                           r=["psT"], w=["xnTg%d" % (gi % 2)])

                pro_norm(0)
                pro_T(0)
                for gi, tiles in enumerate(groups):
                    ntok = 128 * len(tiles)
                    xb = xnT[gi % 2]
                    xkey = "xnTg%d" % (gi % 2)
                    for f in range(32):
                        cnt["P"] += 1
                        pb = cnt["P"] % 3
                        for kc in range(8):
                            mm(psP[pb][:, 0:ntok], W1[:, kc, f * 128:(f + 1) * 128], xb[:, kc, 0:ntok], kc == 0, kc == 7,
                               r=[xkey, "W1"], bank=("psP", pb), first=(kc == 0), last=(kc == 7))
                        cnt["rl"] += 1
                        rk = cnt["rl"] % 2
                        act(rl[rk][:, 0:ntok], psP[pb][:, 0:ntok], AF.Relu, r=[("psP", pb)], w=[("rl", rk)])
                        tt("pool" if f % 2 else "dve", hT[:, f, 0:ntok], rl[rk][:, 0:ntok], rl[rk][:, 0:ntok], ALU.mult,
                           r=[("rl", rk)], w=["hT"])
                    if gi + 1 < len(groups):
                        pro_norm(gi + 1)
                    for t, ti in enumerate(tiles):
                        slot = (gi % 2) * GT + t
                        for half in range(2):
                            cnt["Y"] += 1
                            yb = cnt["Y"] % 4
                            for f in range(32):
                                mm(psY[yb][:, :], hT[:, f, t * 128:(t + 1) * 128], W2[:, f, half * 512:(half + 1) * 512],
                                   f == 0, f == 31, r=["hT", "W2"], bank=("psY", yb), first=(f == 0), last=(f == 31))
                            tt("dve", xt[slot][:, half * 512:(half + 1) * 512], psY[yb][:, :],
                               xt[slot][:, half * 512:(half + 1) * 512], ALU.add,
                               r=[("psY", yb), ("xt", slot)], w=[("xt", slot)])
                        if not final:
                            S.dma("sp", Hout[ti * 128:(ti + 1) * 128, :], xt[slot][:, :], r=[("xt", slot)], w=[],
                                  key=("xt", slot))
                        else:
                            cnt["yo"] += 1
                            yk = cnt["yo"] % 2
                            act(yo[yk][:, :], xt[slot][:, :], AF.Square, r=[("xt", slot)], w=[("yo", yk), "ssf"], accum_out=ssf[:, :])
                            rsqrt_(ssf[:, :], 128, eps_rms, 1.0 / D, "ssf")
                            stt(yo[yk][:, :], xt[slot][:, :], ssf[:, :], gfin[:, :], ALU.mult, ALU.mult,
                                r=[("xt", slot), "ssf", "gfin"], w=[("yo", yk)])
                            S.dma("sp", Hout[ti * 128:(ti + 1) * 128, :], yo[yk][:, :], r=[("yo", yk)], w=[],
                                  key=("yo", yk))
                    if gi + 1 < len(groups):
                        pro_T(gi + 1)
                S.barrier()

        def phase_B0(Hin):
            with contextlib.ExitStack() as st:
                def sb(name, shape, dt):
                    return st.enter_context(nc.sbuf_tensor("B0_" + name, list(shape), dt))
                Wg = sb("Wg", [128, 8, 2048], BF16)
                gain = sb("gain", [128, D], F32)
                xt = [sb("xt%d" % i, [128, D], F32) for i in range(2)]
                xn = [sb("xn%d" % i, [128, D], BF16) for i in range(4)]
                ss = [sb("ss%d" % i, [128, 1], F32) for i in range(4)]
                xnT = [sb("xnT%d" % i, [128, 8, 512], BF16) for i in range(2)]
                sg = [sb("sg%d" % i, [128, 4, 16, 128], BF16) for i in range(2)]
                S.dma("pool", Wg[:, :, :], I.ml_w_in[:, 2048:4096].rearrange("(k p) n -> p k n", p=128), w=["Wg"], key="Wg")
                S.dma("sp", gain[:, :], I.norm_mix[1].partition_broadcast(128), w=["gain"], key="gain")
                psT, psP = ps[0], [ps[1], ps[2], ps[3]]
                cnt = {"P": 0, "xt": 0}
                groups = [list(range(4 * g, 4 * g + 4)) for g in range(NTI // 4)] + [[NTI]]

                def pro_norm(gi):
                    for t, ti in enumerate(groups[gi]):
                        cnt["xt"] += 1
                        xk = cnt["xt"] % 2
                        S.dma("sp", xt[xk][:, :], Hin[ti * 128:(ti + 1) * 128, :], w=[("xt", xk)], key=("xt", xk))
                        act(xn[t][:, :], xt[xk][:, :], AF.Square, r=[("xt", xk)], w=[("xn", t), ("ss", t)], accum_out=ss[t][:, :])
                        rsqrt_(ss[t][:, :], 128, eps_rms, 1.0 / D, ("ss", t))
                        stt(xn[t][:, :], xt[xk][:, :], ss[t][:, :], gain[:, :], ALU.mult, ALU.mult,
                            r=[("xt", xk), ("ss", t), "gain"], w=[("xn", t)])

                def pro_T(gi):
                    xb = xnT[gi % 2]
                    pT = psT[:, :].bitcast(BF16)
                    for t, ti in enumerate(groups[gi]):
                        for kc in range(8):
                            tr(pT[:, kc * 128:(kc + 1) * 128], xn[t][:, kc * 128:(kc + 1) * 128], ident_b[:, :],
                               r=[("xn", t), "ident_b"], bank="psT", first=(kc == 0), last=(kc == 7))
                        cp("dve", xb[:, :, t * 128:(t + 1) * 128], pT[:, 0:1024].rearrange("p (k t) -> p k t", k=8),
                           r=["psT"], w=["xnTg%d" % (gi % 2)])

                pro_norm(0)
                pro_T(0)
                for gi, tiles in enumerate(groups):
                    ntok = 128 * len(tiles)
                    xb = xnT[gi % 2]
                    xkey = "xnTg%d" % (gi % 2)
                    sgb = sg[gi % 2]
                    skey = "sg%d" % (gi % 2)
                    for oc in range(16):
                        if oc == 4 and gi + 1 < len(groups):
                            pro_norm(gi + 1)
                        cnt["P"] += 1
                        pb = cnt["P"] % 3
                        for kc in range(8):
                            mm(psP[pb][:, 0:ntok], Wg[:, kc, oc * 128:(oc + 1) * 128], xb[:, kc, 0:ntok], kc == 0, kc == 7,
                               r=[xkey, "Wg"], bank=("psP", pb), first=(kc == 0), last=(kc == 7))
                        act(sgb[:, 0:len(tiles), oc, :], psP[pb][:, 0:ntok].rearrange("p (t k) -> p t k", k=128),
                            AF.Sigmoid, r=[("psP", pb)], w=[skey])
                    if gi + 1 < len(groups):
                        pro_T(gi + 1)
                    for t, ti in enumerate(tiles):
                        S.dma("sp", SG[ti], sgb[:, t, :, :].rearrange("p c k -> p (c k)"), r=[skey], w=[], key=skey)
                S.barrier()

        def phase_B1(Hin):
            with contextlib.ExitStack() as st:
                def sb(name, shape, dt, stack=st):
                    return stack.enter_context(nc.sbuf_tensor("B1_" + name, list(shape), dt))
                Wx = sb("Wx", [128, 8, 2048], BF16)
                BD = sb("BD", [128, 3, 16, 128], BF16)
                wif = sb("wif", [128, 48, 8], BF16)
                wcv = sb("wcv", [128, 16, 4], F32)
                bcv = sb("bcv", [128, 16], F32)
                gnm = sb("gnm", [128, 16], F32)
                skp = sb("skp", [128, 16], F32)
                gain = sb("gain", [128, D], F32)
                Eh = sb("Eh", [4, 4, 128], F32)
                NEG = sb("NEG", [128, 128], BF16)
                ones4 = sb("ones4", [4, 128], F32)
                onesb = sb("onesb", [128, 1], BF16)
                bi = sb("bi", [4, 1], F32)
                nbf_ = sb("nbf_", [4, 1], F32)
                one41 = sb("one41", [4, 1], F32)
                lnsc = sb("lnsc", [128, 1], F32)
                C = sb("C", [128, 4, 4, 512], F32)
                Cb = sb("Cb", [128, 4, 4, 512], BF16)
                nst = sb("nst", [128, 4, 4], F32)
                nstb = sb("nstb", [128, 4, 4], BF16)
                mcar = sb("mcar", [4, 1], F32)
                xt = [sb("xt%d" % i, [128, D], F32) for i in range(2)]
                xn = sb("xn", [128, D], BF16)
                ss = sb("ss", [128, 1], F32)
                xnT = sb("xnT", [128, 8, 128], BF16)
                xmT = sb("xmT", [128, 16, 3 + 128], F32)
                xmb = [sb("xmb%d" % i, [128, 16, 128], BF16) for i in range(2)]
                cacc = [sb("cacc%d" % i, [128, 128], F32) for i in range(4)]
                caT = [sb("caT%d" % i, [128, 16, 128], BF16) for i in range(2)]
                qT = [sb("qT%d" % i, [128, 16, 128], BF16) for i in range(2)]
                kT = [sb("kT%d" % i, [128, 16, 128], BF16) for i in range(2)]
                vT = sb("vT", [128, 16, 128], BF16)
                ktm = sb("ktm", [128, 2048], BF16)
                vtm = sb("vtm", [128, 2048], BF16)
                sgt = sb("sgt", [128, 4, 128], BF16)
                gpre = [sb("gpre%d" % i, [4, 256], F32) for i in range(2)]
                gts = [{k: sb("g%d_" % i + k, [4, 128], F32) for k in ("ip", "ef", "sp", "ft", "mt", "nF", "u", "w", "si", "em")}
                       for i in range(2)]
                gzero = sb("gzero", [4, 128], F32)
                dg = sb("dg", [4, 4], F32)
                cols = [sb("cols%d" % i, [128, 8], F32) for i in range(2)]
                scB = [sb("scB%d" % i, [128, 4], F32) for i in range(2)]
                Dp4 = sb("Dp4", [128, 4, 128], F32)
                aT4 = sb("aT4", [128, 4, 128], BF16)
                qs = sb("qs", [128, 16, 128], BF16)
                sib = sb("sib", [128, 4, 128], BF16)
                bst4 = sb("bst4", [128, 4, 6], F32)
                mv4 = sb("mv4", [128, 4, 2], F32)
                den4 = sb("den4", [128, 4], F32)
                rd4 = sb("rd4", [128, 4], F32)
                t4a = sb("t4a", [128, 4], F32)
                t4b = sb("t4b", [128, 4], F32)
                A4 = sb("A4", [128, 4], F32)
                B4 = sb("B4", [128, 4], F32)
                hn = [sb("hn%d" % i, [128, 512], BF16) for i in range(2)]
                hnT = sb("hnT", [128, 16, 128], BF16)
                wk = [sb("wk%d" % i, [128, 512], BF16) for i in range(2)]
                y1 = sb("y1", [128, 4, 128], F32)
                y2 = sb("y2", [128, 4, 128], F32)
                yT = sb("yT", [128, 16, 128], BF16)
                xe = sb("xe", [128, 16, 3 + TS], F32)

                S.dma("pool", Wx[:, :, :], I.ml_w_in[:, 0:2048].rearrange("(k p) n -> p k n", p=128), w=["Wx"], key="Wx")
                S.dma("pool", wif[:, :, 0:4], I.ml_w_i.rearrange("(c p) h -> p c h", p=128), w=["wif0"], key="wif")
                S.dma("pool", wif[:, :, 4:8], I.ml_w_f.rearrange("(c p) h -> p c h", p=128), w=["wif1"], key="wif")
                for j in range(4):
                    S.dma("sp", wcv[:, :, j], I.ml_w_conv[j].rearrange("(c p) -> p c", p=128), w=["wcv"], key="wcv",
                          allow_slow_non_contiguous=True)
                for dst, src, kk in ((bcv, I.ml_b_conv, "bcv"), (gnm, I.ml_norm, "gnm"), (skp, I.ml_skip, "skp")):
                    S.dma("sp", dst[:, :], src.rearrange("(c p) -> p c", p=128), w=[kk], key=kk, allow_slow_non_contiguous=True)
                S.dma("sp", gain[:, :], I.norm_mix[1].partition_broadcast(128), w=["gain"], key="gain")
                S.dma("sp", Eh[:, :, :], I.c_eh.rearrange("h k s -> k h s"), w=["Eh"], key="Eh")
                S.dma("pool", NEG[:, :], I.c_neg[:, :], w=["NEG"], key="NEG")
                S.dma("sp", bi[:, :], I.ml_b_i.rearrange("(h o) -> h o", o=1), w=["bi"], key="bi")
                S.dma("sp", nbf_[:, :], I.ml_b_f.rearrange("(h o) -> h o", o=1), w=["nbf"], key="nbf")
                tsc("dve", nbf_[:, :], nbf_[:, :], -1.0, ALU.mult, r=["nbf"], w=["nbf"])
                mset("dve", ones4[:, :], 1.0, w=["ones4"])
                mset("dve", onesb[:, :], 1.0, w=["onesb"])
                mset("dve", one41[:, :], 1.0, w=["one41"])
                mset("dve", lnsc[:, :], math.log(512.0 ** -0.5), w=["lnsc"])
                mset("dve", gzero[:, :], 0.0, w=["g_zero"])
                with contextlib.ExitStack() as stw:
                    Wexp = sb("Wexp", [128, 3, 16, 4], F32, stw)
                    bdm = sb("bdm", [128, 32], F32, stw)
                    for j, wsrc in enumerate((I.ml_w_q, I.ml_w_k, I.ml_w_v)):
                        S.dma("sp", Wexp[:, j, :, :], wsrc.rearrange("(c g) i o -> (g i) c o", g=32), w=["Wexp"], key="Wexp")
                    S.dma("sp", bdm[:, :], I.c_bd[:, :], w=["bdm"], key="bdm")
                    for j in range(3):
                        for c in range(16):
                            tt("dve", BD[:, j, c, :].rearrange("p (g o) -> p g o", o=4),
                               Wexp[:, j, c, :].unsqueeze(1).broadcast_to([128, 32, 4]),
                               bdm[:, :].unsqueeze(2).broadcast_to([128, 32, 4]), ALU.mult,
                               r=["Wexp", "bdm"], w=["BD"])
                    S.barrier()

                psT, psF = ps[0], ps[1]
                psA, psS, psN0, psN1, psD = ps[2], ps[3], ps[4], ps[5], ps[6]
                FB = [(ps[0], "psT"), (ps[1], "psF"), (ps[7], "psF2")]
                cnt = {"xt": 0, "F": 0, "K": 0}

                def nextF():
                    cnt["F"] += 1
                    return FB[cnt["F"] % 3]

                def load_x(rows_ap):
                    cnt["xt"] += 1
                    xk = cnt["xt"] % 2
                    S.dma("sp", xt[xk][:, :], rows_ap, w=[("xt", xk)], key=("xt", xk))
                    norm_T(xt[xk][:, :], gain[:, :], xn[:, :], ss[:, :], xnT[:, :, :],
                           {"xt": ("xt", xk), "xn": "xn", "ss": "ss", "xnT": "xnT"}, psT)

                def proj_xm(p):
                    for c4 in range(4):
                        fb, fk = nextF()
                        for cc in range(4):
                            c = c4 * 4 + cc
                            for kc in range(8):
                                mm(fb[:, cc * 128:(cc + 1) * 128], Wx[:, kc, c * 128:(c + 1) * 128], xnT[:, kc, :],
                                   kc == 0, kc == 7, r=["xnT", "Wx"], bank=fk,
                                   first=(kc == 0 and cc == 0), last=(kc == 7 and cc == 3))
                        cp("act", xmT[:, c4 * 4:(c4 + 1) * 4, 3:131], fb[:, :].rearrange("p (c k) -> p c k", c=4),
                           r=[fk], w=["xmT"])
                        cp("dve", xmb[p][:, c4 * 4:(c4 + 1) * 4, :], fb[:, :].rearrange("p (c k) -> p c k", c=4),
                           r=[fk], w=[("xmb", p)])
                        yield

                def conv_silu(ext_fn, ncol, ca_dst_fn, rkey, cakey):
                    for c8 in range(0, 16, 4):
                        for j in range(4):
                            for cc in range(4):
                                c = c8 + cc
                                e = ext_fn(c)
                                if j == 0:
                                    tsc("dve", cacc[cc][:, 0:ncol], e[:, 0:ncol], wcv[:, c, 0:1], ALU.mult, r=[rkey, "wcv"],
                                        w=[("cacc", cc)])
                                else:
                                    stt(cacc[cc][:, 0:ncol], e[:, j:j + ncol], wcv[:, c, j:j + 1], cacc[cc][:, 0:ncol],
                                        ALU.mult, ALU.add, r=[rkey, "wcv", ("cacc", cc)], w=[("cacc", cc)])
                            yield
                        for cc in range(4):
                            c = c8 + cc
                            act(ca_dst_fn(c), cacc[cc][:, 0:ncol], AF.Silu, r=[("cacc", cc), "bcv"], w=[cakey],
                                bias=bcv[:, c:c + 1])
                        yield

                def qkv_gates(p):
                    for j, (src, dst, key, skey) in enumerate(((caT[p], qT[p], ("qT", p), ("caT", p)),
                                                               (caT[p], kT[p], ("kT", p), ("caT", p)),
                                                               (xmb[p], vT, "vT", ("xmb", p)))):
                        for c4 in range(4):
                            fb, fk = nextF()
                            for cc in range(4):
                                c = c4 * 4 + cc
                                mm(fb[:, cc * 128:(cc + 1) * 128], BD[:, j, c, :], src[:, c, :], True, True,
                                   r=["BD", skey], bank=fk, first=(cc == 0), last=(cc == 3))
                            cp("act" if c4 % 2 else "dve", dst[:, c4 * 4:(c4 + 1) * 4, :],
                               fb[:, :].rearrange("p (c k) -> p c k", c=4), r=[fk], w=[key])
                            yield
                    for gsel in range(2):
                        for ch in range(48):
                            srcT = (qT[p], kT[p], vT)[ch // 16]
                            mm(psF[0:4, gsel * 128:(gsel + 1) * 128], wif[:, ch, gsel * 4:gsel * 4 + 4], srcT[:, ch % 16, :],
                               ch == 0, ch == 47, r=["wif0", "wif1", ("qT", p), ("kT", p), "vT"], bank="psF",
                               first=(ch == 0 and gsel == 0), last=(ch == 47 and gsel == 1))
                    cp("dve", gpre[p][:, :], psF[0:4, 0:256], r=["psF"], w=[("gpre", p)])
                    yield

                def gate_math(L, c0, p):
                    g = gts[p]
                    G = lambda k: ("g_" + k, p)
                    sl = slice(0, L)
                    tsc("dve", g["ip"][:, sl], gpre[p][:, c0:c0 + L], bi[:, 0:1], ALU.add, r=[("gpre", p), "bi"], w=[G("ip")])
                    act(g["ef"][:, sl], gpre[p][:, 128 + c0:128 + c0 + L], AF.Exp, r=[("gpre", p), "nbf"], w=[G("ef")],
                        bias=nbf_[:, 0:1], scale=-1.0)
                    yield
                    act(g["sp"][:, sl], g["ef"][:, sl], AF.Ln, r=[G("ef"), "one41"], w=[G("sp")], bias=one41[:, 0:1])
                    yield
                    tsc("dve", g["ft"][:, sl], g["sp"][:, sl], -1.0, ALU.mult, r=[G("sp")], w=[G("ft")])
                    S.op("dve", lambda e: e.tensor_tensor_scan(out=g["nF"][:, sl], data0=g["sp"][:, sl], data1=gzero[:, sl],
                                                               initial=0.0, op0=ALU.add, op1=ALU.add),
                         r=[G("sp"), "g_zero"], w=[G("nF")])
                    yield
                    S.op("dve", lambda e: e.tensor_tensor_scan(out=g["mt"][:, sl], data0=g["ft"][:, sl], data1=g["ip"][:, sl],
                                                               initial=mcar[:, 0:1], op0=ALU.add, op1=ALU.max),
                         r=[G("ft"), G("ip"), "mcar"], w=[G("mt")])
                    tt("dve", g["u"][:, sl], g["ip"][:, sl], g["nF"][:, sl], ALU.add, r=[G("ip"), G("nF")], w=[G("u")])
                    yield
                    stt(g["w"][:, sl], g["nF"][:, sl], -1.0, g["mt"][:, sl], ALU.mult, ALU.subtract, r=[G("nF"), G("mt")],
                        w=[G("w")])
                    act(g["em"][:, sl], g["mt"][:, sl], AF.Exp, r=[G("mt")], w=[G("em")], scale=-2.0)
                    yield
                    act(g["si"][:, sl], g["w"][:, sl], AF.Exp, r=[G("w"), "mcar"], w=[G("si")], bias=mcar[:, 0:1])
                    yield
                    cp("dve", mcar[:, :], g["mt"][:, L - 1:L], r=[G("mt")], w=["mcar"])
                    tr(psF[0:L, 0:4], g["si"][:, sl], ident_f[0:4, 0:4], r=[G("si"), "ident_f"], bank="psF", first=True)
                    tr(psF[0:L, 4:8], g["em"][:, sl], ident_f[0:4, 0:4], r=[G("em"), "ident_f"], bank="psF", last=True)
                    cp("dve", cols[p][0:L, :], psF[0:L, 0:8], r=["psF"], w=[("cols", p)])
                    tsc("dve", dg[:, :], ident_f[0:4, 0:4], g["si"][:, L - 1:L], ALU.mult, r=["ident_f", G("si")], w=["dg"])
                    yield
                    mm(psF[:, 8:12], ones4[:, :], dg[:, :], True, True, r=["ones4", "dg"], bank="psF", first=True, last=True)
                    cp("dve", scB[p][:, :], psF[:, 8:12], r=["psF"], w=[("scB", p)])
                    yield

                def cell(L, c0, p, hn_dst_fn):
                    g = gts[p]
                    G = lambda k: ("g_" + k, p)
                    NB = [psN0, psN1, psA, psS]
                    NBk = ["psN0", "psN1", "psA", "psS"]
                    qTp, kTp, caTp, xmbp, colsp, scBp = qT[p], kT[p], caT[p], xmb[p], cols[p], scB[p]
                    for j, (src, dstm, key, skey) in enumerate(((caTp, ktm, "ktm", ("caT", p)), (xmbp, vtm, "vtm", ("xmb", p)))):
                        for c4 in range(4):
                            kb, kbk = NB[c4], NBk[c4]
                            for cc in range(4):
                                c = c4 * 4 + cc
                                mm(kb[0:L, cc * 128:(cc + 1) * 128], src[:, c, c0:c0 + L], BD[:, 1 + j, c, :], True, True,
                                   r=["BD", skey], bank=kbk, first=(cc == 0), last=(cc == 3))
                            cp("act" if c4 % 2 else "dve", dstm[0:L, c4 * 512:(c4 + 1) * 512], kb[0:L, :],
                               r=[kbk], w=[key])
                            yield
                    for h in range(4):
                        mm(psA[0:L, h * 128:h * 128 + L], g["u"][:, 0:L], Eh[:, h, 0:L], True, False, r=[G("u"), "Eh"],
                           bank="psA", first=(h == 0))
                        mm(psA[0:L, h * 128:h * 128 + L], Eh[:, h, 0:L], g["w"][:, 0:L], False, False, r=[G("w"), "Eh"])
                        mm(psA[0:L, h * 128:h * 128 + L], ident_b[0:L, 0:L], NEG[0:L, 0:L], False, True, r=["ident_b", "NEG"],
                           bank="psA", last=(h == 3), inc=True)
                    yield
                    for h in range(4):
                        for dc in range(4):
                            mm(psS[0:L, h * 128:h * 128 + L], kTp[:, 4 * h + dc, c0:c0 + L], qTp[:, 4 * h + dc, c0:c0 + L],
                               dc == 0, dc == 3, r=[("kT", p), ("qT", p)], bank="psS", first=(h == 0 and dc == 0),
                               last=(h == 3 and dc == 3), inc=(dc == 3))
                    yield
                    A4v = psA[0:L, :].rearrange("p (h s) -> p h s", h=4)[:, :, 0:L]
                    G4v = psS[0:L, :].rearrange("p (h s) -> p h s", h=4)[:, :, 0:L]
                    act(Dp4[0:L, :, 0:L], A4v, AF.Exp, r=["psA", "lnsc"], w=["Dp4"], bias=lnsc[0:L, 0:1])
                    yield
                    for h in range(4):
                        mm(psA[:, h * 128:h * 128 + L], Eh[:, h, :], g["si"][:, 0:L], True, True, r=["Eh", G("si")],
                           bank="psA", first=(h == 0), last=(h == 3), inc=(h == 3))
                    yield
                    tt("dve", aT4[0:L, :, 0:L], Dp4[0:L, :, 0:L], G4v, ALU.mult, r=["Dp4", "psS"], w=["aT4"])
                    yield
                    sbv = psA[:, :].rearrange("p (h s) -> p h s", h=4)[:, :, 0:L]
                    cp("act", sib[:, :, 0:L], sbv, r=["psA"], w=["sib"])
                    yield
                    tt("dve", qs[:, :, 0:L].rearrange("p (h d) s -> p h d s", h=4),
                       qTp[:, :, c0:c0 + L].rearrange("p (h d) s -> p h d s", h=4),
                       sib[:, :, 0:L].unsqueeze(2).broadcast_to([128, 4, 4, L]), ALU.mult, r=[("qT", p), "sib"], w=["qs"])
                    yield
                    for h in range(4):
                        for dc in range(4):
                            mm(NB[h][0:L, :], qs[:, 4 * h + dc, 0:L], Cb[:, h, dc, :], dc == 0, False,
                               r=["qs", ("Cb", h)], bank=NBk[h], first=(dc == 0))
                        mm(NB[h][0:L, :], aT4[0:L, h, 0:L], vtm[0:L, h * 512:(h + 1) * 512], False, True, r=["aT4", "vtm"],
                           bank=NBk[h], last=True)
                        for dc in range(4):
                            mm(psD[0:L, 16 + h:17 + h], qs[:, 4 * h + dc, 0:L], nstb[:, h, dc:dc + 1], dc == 0, False,
                               r=["qs", ("nstb", h)], bank="psD", first=(h == 0 and dc == 0))
                        mm(psD[0:L, 16 + h:17 + h], aT4[0:L, h, 0:L], onesb[0:L, :], False, True, r=["aT4", "onesb"],
                           bank="psD", last=(h == 3), inc=True)
                        yield
                    for h in range(4):
                        S.op("dve", lambda e: e.bn_stats(out=bst4[0:L, h, :], in_=NB[h][0:L, :]), r=[NBk[h]], w=["bst4"])
                        S.op("dve", lambda e: e.bn_aggr(out=mv4[0:L, h, :], in_=bst4[0:L, h, :]), r=["bst4"], w=["mv4"])
                        yield
                    cp("dve", den4[0:L, :], psD[0:L, 16:20], r=["psD"], w=["den4"])
                    yield
                    tt("dve", den4[0:L, :], den4[0:L, :], den4[0:L, :], ALU.mult, r=["den4"], w=["den4"])
                    yield
                    tt("dve", den4[0:L, :], den4[0:L, :], colsp[0:L, 4:8], ALU.max, r=["den4", ("cols", p)], w=["den4"])
                    yield
                    stt(t4b[0:L, :], den4[0:L, :], LN_EPS, mv4[0:L, :, 1], ALU.mult, ALU.add, r=["den4", "mv4"], w=["t4b"])
                    yield
                    act(t4b[0:L, :], t4b[0:L, :], AF.Sqrt, r=["t4b"], w=["t4b"])
                    yield
                    S.op("dve", lambda e: e.reciprocal(out=A4[0:L, :], in_=t4b[0:L, :]), r=["t4b"], w=["A4"])
                    yield
                    stt(B4[0:L, :], mv4[0:L, :, 0], -1.0, A4[0:L, :], ALU.mult, ALU.mult, r=["mv4", "A4"], w=["B4"])
                    yield
                    for h in range(4):
                        pT = NB[h][:, :].bitcast(BF16)
                        hk = h % 2
                        if h % 2 == 0:
                            act(hn[hk][0:L, :], NB[h][0:L, :], AF.Identity, r=[NBk[h], "A4", "B4"], w=[("hn", hk)],
                                bias=B4[0:L, h:h + 1], scale=A4[0:L, h:h + 1])
                        else:
                            tsc("dve", hn[hk][0:L, :], NB[h][0:L, :], A4[0:L, h:h + 1], ALU.mult, B4[0:L, h:h + 1], ALU.add,
                                r=[NBk[h], "A4", "B4"], w=[("hn", hk)])
                        yield
                        for dc in range(4):
                            tr(pT[:, dc * L:(dc + 1) * L], hn[hk][0:L, dc * 128:(dc + 1) * 128], ident_b[0:L, 0:L],
                               r=[("hn", hk), "ident_b"], bank=NBk[h], first=(dc == 0), last=(dc == 3))
                        cp("act" if h % 2 else "dve", hn_dst_fn(h), pT[:, 0:4 * L].rearrange("p (c q) -> p c q", c=4),
                           r=[NBk[h]], w=["hnT"])
                        yield
                    for h in range(4):
                        wkk = h % 2
                        if h % 2 == 0:
                            tsc("dve", wk[wkk][0:L, :], ktm[0:L, h * 512:(h + 1) * 512], Dp4[0:L, h, L - 1:L], ALU.mult,
                                r=["ktm", "Dp4"], w=[("wk", wkk)])
                        else:
                            act(wk[wkk][0:L, :], ktm[0:L, h * 512:(h + 1) * 512], AF.Copy, r=["ktm", "Dp4"], w=[("wk", wkk)],
                                scale=Dp4[0:L, h, L - 1:L])
                        yield
                        for dc in range(4):
                            mm(psD[:, 24 + dc:25 + dc], wk[wkk][0:L, dc * 128:(dc + 1) * 128], onesb[0:L, :], True, True,
                               r=[("wk", wkk), "onesb"], bank="psD", first=(dc == 0), last=(dc == 3))
                        stt(nst[:, h, :], nst[:, h, :], scBp[:, h:h + 1], psD[:, 24:28], ALU.mult, ALU.add,
                            r=["psD", ("scB", p), ("nst", h)], w=[("nst", h)])
                        cp("pool", nstb[:, h, :], nst[:, h, :], r=[("nst", h)], w=[("nstb", h)])
                        yield
                        for dc in range(4):
                            bi_ = (4 * h + dc) % 4
                            bk, bkey = NB[bi_], NBk[bi_]
                            mm(bk[:, :], wk[wkk][0:L, dc * 128:(dc + 1) * 128], vtm[0:L, h * 512:(h + 1) * 512], True, True,
                               r=[("wk", wkk), "vtm"], bank=bkey, first=True, last=True)
                            stt(C[:, h, dc, :], C[:, h, dc, :], scBp[:, h:h + 1], bk[:, :], ALU.mult, ALU.add,
                                r=[bkey, ("scB", p), ("C", (h, dc))], w=[("C", (h, dc))])
                            cp("act", Cb[:, h, dc, :], C[:, h, dc, :], r=[("C", (h, dc))], w=[("Cb", h)])
                            yield

                def finish_tile(ti, p):
                    for c4 in range(4):
                        cs = slice(c4 * 4, c4 * 4 + 4)
                        S.dma("sp", sgt[:, :, :].rearrange("p c k -> p (c k)"), SG[ti, :, c4 * 512:(c4 + 1) * 512], w=["sgt"], key="sgt")
                        tt("dve", y1[:, :, :], hnT[:, cs, :], gnm[:, cs].unsqueeze(2).broadcast_to([128, 4, 128]), ALU.mult,
                           r=["hnT", "gnm"], w=["y1"])
                        tt("pool", y2[:, :, :], caT[p][:, cs, :], skp[:, cs].unsqueeze(2).broadcast_to([128, 4, 128]), ALU.mult,
                           r=[("caT", p), "skp"], w=["y2"])
                        yield
                        tt("pool", y1[:, :, :], y1[:, :, :], y2[:, :, :], ALU.add, r=["y1", "y2"], w=["y1"])
                        yield
                        tt("pool", yT[:, cs, :], y1[:, :, :], sgt[:, :, :], ALU.mult, r=["y1", "sgt"], w=["yT"])
                        yield
                    S.dma("sp", YT[ti], yT[:, :, :].rearrange("p c k -> p (c k)"), r=["yT"], w=[], key="yT")
                    yield

                def drain(gen):
                    for _ in gen:
                        pass

                def interleave(gens):
                    gens = [g_ for g_ in gens if g_ is not None]
                    while gens:
                        for g_ in list(gens):
                            try:
                                next(g_)
                            except StopIteration:
                                gens.remove(g_)

                def FE(ti):
                    p = ti % 2
                    load_x(Hin[ti * 128:(ti + 1) * 128, :])
                    yield
                    yield from proj_xm(p)
                    yield from conv_silu(lambda c: xmT[:, c, :], 128, lambda c: caT[p][:, c, :], "xmT", ("caT", p))
                    if ti == NTI - 1:
                        for j in range(3):
                            S.dma("sp", O.pconv[j].rearrange("(c p) -> p c", p=128), xmT[:, :, 128 + j], r=["xmT"], w=[],
                                  key="xmTo", allow_slow_non_contiguous=True)
                    cp("dve", xmT[:, :, 0:3], xmT[:, :, 128:131], r=["xmT"], w=["xmT"])
                    yield
                    yield from qkv_gates(p)
                    yield from gate_math(128, 0, p)

                def BE(ti):
                    p = ti % 2
                    yield from cell(128, 0, p, lambda h: hnT[:, 4 * h:4 * h + 4, :])
                    yield from finish_tile(ti, p)

                mset("dve", C[:, :, :, :], 0.0, w=[("C", (h, dc)) for h in range(4) for dc in range(4)])
                mset("pool", Cb[:, :, :, :], 0.0, w=[("Cb", h) for h in range(4)])
                mset("dve", nst[:, :, :], 0.0, w=[("nst", h) for h in range(4)])
                mset("dve", nstb[:, :, :], 0.0, w=[("nstb", h) for h in range(4)])
                mset("dve", mcar[:, :], 0.0, w=["mcar"])
                mset("dve", xmT[:, :, 0:3], 0.0, w=["xmT"])
                drain(FE(0))
                for ti in range(NTI):
                    interleave([BE(ti), FE(ti + 1) if ti + 1 < NTI else None])
                for h in range(4):
                    S.dma("sp", O.pC[h].rearrange("(dc p) e -> p dc e", p=128), C[:, h, :, :], r=[("C", (h, dc)) for dc in range(4)], w=[], key=("Co", h))
                for h in range(4):
                    S.dma("sp", O.pn[h].rearrange("(dc p) -> p dc", p=128), nst[:, h, :], r=[("nst", h)], w=[],
                          key="nsto", allow_slow_non_contiguous=True)
                S.dma("sp", O.pm.rearrange("(h o) -> h o", o=1), mcar[:, :], r=["mcar"], w=[], key="mcaro")

                load_x(Hin[NT:NT + 128, :])
                drain(proj_xm(0))
                mset("dve", caT[0][:, :, :], 0.0, w=[("caT", 0)])
                mset("dve", hnT[:, :, :], 0.0, w=["hnT"])
                for n in range(NSEQ):
                    c0 = n * TS
                    for j in range(3):
                        S.dma("sp", xe[:, :, j], I.sconv[n, j].rearrange("(c p) -> p c", p=128), w=["xe"], key="xe",
                              allow_slow_non_contiguous=True)
                    cp("dve", xe[:, :, 3:3 + TS], xmT[:, :, 3 + c0:3 + c0 + TS], r=["xmT"], w=["xe"])
                    drain(conv_silu(lambda c: xe[:, c, :], TS, lambda c: caT[0][:, c, c0:c0 + TS], "xe", ("caT", 0)))
                    for j in range(3):
                        S.dma("sp", O.sconv[n, j].rearrange("(c p) -> p c", p=128), xe[:, :, TS + j], r=["xe"], w=[],
                              key="xe", allow_slow_non_contiguous=True)
                drain(qkv_gates(0))
                for n in range(NSEQ):
                    c0 = n * TS
                    for h in range(4):
                        S.dma("sp", C[:, h, :, :], I.sC[n, h].rearrange("(dc p) e -> p dc e", p=128), w=[("C", (h, dc)) for dc in range(4)], key=("Co", h))
                        cp("act", Cb[:, h, :, :], C[:, h, :, :], r=[("C", (h, dc)) for dc in range(4)], w=[("Cb", h)])
                    for h in range(4):
                        S.dma("sp", nst[:, h, :], I.sn[n, h].rearrange("(dc p) -> p dc", p=128), w=[("nst", hh) for hh in range(4)],
                              key="nsto", allow_slow_non_contiguous=True)
                    for h in range(4):
                        cp("pool", nstb[:, h, :], nst[:, h, :], r=[("nst", h)], w=[("nstb", h)])
                    S.dma("sp", mcar[:, :], I.sm[n].rearrange("(h o) -> h o", o=1), w=["mcar"], key="mcaro")
                    drain(gate_math(TS, c0, 0))
                    drain(cell(TS, c0, 0, lambda h: hnT[:, 4 * h:4 * h + 4, c0:c0 + TS]))
                    for h in range(4):
                        S.dma("sp", O.sC[n, h].rearrange("(dc p) e -> p dc e", p=128), C[:, h, :, :], r=[("C", (h, dc)) for dc in range(4)], w=[],
                              key=("Co", h))
                    for h in range(4):
                        S.dma("sp", O.sn[n, h].rearrange("(dc p) -> p dc", p=128), nst[:, h, :], r=[("nst", h)],
                              w=[], key="nsto", allow_slow_non_contiguous=True)
                    S.dma("sp", O.sm[n].rearrange("(h o) -> h o", o=1), mcar[:, :], r=["mcar"], w=[], key="mcaro")
                drain(finish_tile(NTI, 0))
                S.barrier()

        def phase_B2(Hin, Hout):
            with contextlib.ExitStack() as st:
                def sb(name, shape, dt):
                    return st.enter_context(nc.sbuf_tensor("B2_" + name, list(shape), dt))
                Wo = sb("Wo", [128, 16, 1024], BF16)
                yt = [sb("yt%d" % i, [128, 16, 128], BF16) for i in range(3)]
                xr = [sb("xr%d" % i, [128, D], F32) for i in range(3)]
                S.dma("pool", Wo[:, :, :], I.ml_w_out.rearrange("(c p) n -> p c n", p=128), w=["Wo"], key="Wo")
                psY = [ps[i] for i in range(4)]
                cy = 0

                def loads(ti):
                    k = ti % 3
                    S.dma("sp", yt[k][:, :, :].rearrange("p c k -> p (c k)"), YT[ti], w=[("yt", k)], key=("yt", k))
                    S.dma("sp", xr[k][:, :], Hin[ti * 128:(ti + 1) * 128, :], w=[("xr", k)], key=("xr", k))

                loads(0)
                loads(1)
                for ti in range(NTI + 1):
                    k = ti % 3
                    for half in range(2):
                        cy += 1
                        yb = cy % 4
                        for c in range(16):
                            mm(psY[yb][:, :], yt[k][:, c, :], Wo[:, c, half * 512:(half + 1) * 512], c == 0, c == 15,
                               r=[("yt", k), "Wo"], bank=("psY", yb), first=(c == 0), last=(c == 15))
                        tt("dve", xr[k][:, half * 512:(half + 1) * 512], psY[yb][:, :], xr[k][:, half * 512:(half + 1) * 512],
                           ALU.add, r=[("psY", yb), ("xr", k)], w=[("xr", k)])
                    if ti + 2 < NTI + 1:
                        loads(ti + 2)
                    S.dma("sp", Hout[ti * 128:(ti + 1) * 128, :], xr[k][:, :], r=[("xr", k)], w=[], key=("xr", k))
                S.barrier()

        K.stopped = False
        S.barrier()
        if not chk(0):
            phase_A()
        if last_phase >= 1 and not K.stopped:
            phase_FFN(0, H1, H2, False)
        if last_phase >= 2 and not K.stopped:
            phase_B0(H2)
            if not chk(40):
                phase_B1(H2)
                if not chk(41):
                    phase_B2(H2, H3)
        if last_phase >= 3 and not K.stopped:
            phase_FFN(1, H3, O.y, True)
        print("semaphores used:", S.nsem)
    return nc


def _consts():
    ident = np.eye(128, dtype=np.float32)
    mult = np.zeros((128, NDT + 1, 2, 128), np.float32)
    k = np.arange(128)[:, None]
    q = np.arange(128)[None, :]
    for j in range(NDT):
        delta = 16 - j
        dist = 128 * delta + q - k
        m = np.zeros((128, 128), np.float32)
        for win, d in ((128, 1), (512, 4), (2048, 16)):
            m += ((dist >= 0) & (dist % d == 0) & (dist // d <= 128)).astype(np.float32)
        mult[:, j, 0, :] = m
        mult[:, j, 1, :] = m
    eh = np.zeros((4, 4, 128), np.float32)
    for h in range(4):
        eh[h, h, :] = 1.0
    neg = np.where(k <= q, 0.0, NEGBIG).astype(np.float32)
    bd = (np.arange(128)[:, None] // 4 == np.arange(32)[None, :]).astype(np.float32)
    invc = np.zeros((128, 4, 16), np.float32)
    for g, w in enumerate((2, 4, 8, 16)):
        invc[:, g, :] = 1.0 / np.minimum(np.arange(16) + 1.0, float(w))
    return dict(c_ident=ident, c_mult=mult, c_eh=eh, c_neg=neg, c_bd=bd, c_invcnt=invc)


_NC_CACHE = {}


def kernel(**inp):
    f = lambda a: np.ascontiguousarray(np.asarray(a, dtype=np.float32))
    x_prompt = f(inp["x_prompt"])
    x_sample = f(inp["x_sample"])
    consts = _consts()
    shared = dict(
        norm_mix=f(inp["norm_mix"]), norm_ffn=f(inp["norm_ffn"]), norm_final=f(inp["norm_final"]),
        ab_w_in=f(inp["ab_w_in"])[0], ab_w_pool=f(inp["ab_w_pool"])[0], ab_pool_scale=f(inp["ab_pool_scale"])[0],
        ab_w_out=f(inp["ab_w_out"])[0], ml_w_in=f(inp["ml_w_in"])[0], ml_w_conv=f(inp["ml_w_conv"])[0],
        ml_b_conv=f(inp["ml_b_conv"])[0], ml_w_q=f(inp["ml_w_q"])[0], ml_w_k=f(inp["ml_w_k"])[0],
        ml_w_v=f(inp["ml_w_v"])[0], ml_w_i=f(inp["ml_w_i"])[0], ml_b_i=f(inp["ml_b_i"])[0],
        ml_w_f=f(inp["ml_w_f"])[0], ml_b_f=f(inp["ml_b_f"])[0], ml_norm=f(inp["ml_norm"])[0],
        ml_skip=f(inp["ml_skip"])[0], ml_w_out=f(inp["ml_w_out"])[0], ffn_w1=f(inp["ffn_w1"]), ffn_w2=f(inp["ffn_w2"]),
    )
    shared.update(consts)
    ck = f(inp["cache_a_k"])[0].reshape(32, ABUF, 512)
    cv = f(inp["cache_a_v"])[0].reshape(32, ABUF, 512)
    spool = f(inp["state_pool"])[0]
    sC = f(inp["state_ml_C"])[0]
    sn = f(inp["state_ml_n"])[0]
    sm = f(inp["state_ml_m"])[0]
    sconv = f(inp["state_ml_conv"])[0]
    in_maps = []
    for c in range(8):
        b = c % 4
        xin = np.zeros((NROW, D), np.float32)
        xin[:NT] = x_prompt[b]
        xin[NT:NT + NSEQ * TS] = x_sample[4 * c:4 * c + 4].reshape(NSEQ * TS, D)
        m = dict(shared)
        m.update(xin=xin, ck=np.ascontiguousarray(ck[4 * c:4 * c + 4]), cv=np.ascontiguousarray(cv[4 * c:4 * c + 4]),
                 spool=np.ascontiguousarray(spool[4 * c:4 * c + 4]), sC=np.ascontiguousarray(sC[4 * c:4 * c + 4]),
                 sn=np.ascontiguousarray(sn[4 * c:4 * c + 4]), sm=np.ascontiguousarray(sm[4 * c:4 * c + 4]),
                 sconv=np.ascontiguousarray(sconv[4 * c:4 * c + 4]))
        in_maps.append(m)
    if "nc" not in _NC_CACHE:
        _NC_CACHE["nc"] = build_program()
    nc = _NC_CACHE["nc"]
    res = run_bass_kernel_spmd(nc, in_maps, core_ids=list(range(8)))
    R = res.results
    y_prompt = np.stack([R[b]["y"][:NT] for b in range(4)])
    y_sample = np.concatenate([R[c]["y"][NT:NT + 16].reshape(4, 4, D) for c in range(8)])
    p_ak = np.stack([R[b]["pak"].reshape(ABUF, 8, 64) for b in range(4)])[None]
    p_av = np.stack([R[b]["pav"].reshape(ABUF, 8, 64) for b in range(4)])[None]
    p_pool = np.stack([R[b]["ppool"] for b in range(4)])[None]
    p_C = np.stack([R[b]["pC"] for b in range(4)])[None]
    p_n = np.stack([R[b]["pn"] for b in range(4)])[None]
    p_m = np.stack([R[b]["pm"] for b in range(4)])[None]
    p_conv = np.stack([R[b]["pconv"] for b in range(4)])[None]
    s_ak = np.concatenate([R[c]["sak"].reshape(4, 4, 8, 64) for c in range(8)])[None]
    s_av = np.concatenate([R[c]["sav"].reshape(4, 4, 8, 64) for c in range(8)])[None]
    s_pool = np.concatenate([R[c]["spool_o"] for c in range(8)])[None]
    s_C = np.concatenate([R[c]["sC_o"] for c in range(8)])[None]
    s_n = np.concatenate([R[c]["sn_o"] for c in range(8)])[None]
    s_m = np.concatenate([R[c]["sm_o"] for c in range(8)])[None]
    s_conv = np.concatenate([R[c]["sconv_o"] for c in range(8)])[None]
    outs = (y_prompt, y_sample, p_ak, p_av, p_pool, p_C, p_n, p_m, p_conv, s_ak, s_av, s_pool, s_C, s_n, s_m, s_conv)
    return tuple(np.ascontiguousarray(o, dtype=np.float32) for o in outs)
```

```python
import contextlib
import math
import numpy as np
import concourse.bass as bass
import concourse.mybir as mybir
from concourse.bass_utils import run_bass_kernel_spmd

F32 = mybir.dt.float32
BF16 = mybir.dt.bfloat16
AF = mybir.ActivationFunctionType
ALU = mybir.AluOpType
AX = mybir.AxisListType

D = 1024
SEQ = 8192
NT = SEQ
NTI = NT // 128
NROW = NT + 128
NSEQ = 4
TS = 4
ABUF = 2048
PAKR = min(ABUF, NT)
RT = 24
NDT = 17
RMS_EPS = 1e-6
LN_EPS = 1e-5
NEGBIG = -30000.0
LAST_PHASE = 99


class Sched:
    def __init__(self, nc, st):
        self.nc = nc
        self.st = st
        self.E = {"pe": nc.tensor, "act": nc.scalar, "dve": nc.vector, "pool": nc.gpsimd, "sp": nc.sync}
        self.sem = {}
        self.cnt = {}
        self.nsem = 0
        for e in self.E:
            self.sem[e] = self._newsem("e_" + e)
            self.cnt[e] = 0
        self.seen = {e: {} for e in self.E}
        self.lastw = {}
        self.readers = {}
        self.pending = {e: [] for e in self.E}
        self.dsem = {}
        self.dcnt = {}
        self.free_dsems = []
        self.all_dsems = []
        self.pool_sems = set()

    def _newsem(self, name):
        self.nsem += 1
        return self.st.enter_context(self.nc.semaphore(name + "_%d" % self.nsem))

    def _waits(self, eng, reads, writes):
        need = {}

        def add(t):
            if t is None:
                return
            sem, val, owner = t
            if owner == eng and (eng == "pe" or SAME_ENGINE_ORDERED):
                return
            k = id(sem)
            if k not in need or need[k][1] < val:
                need[k] = (sem, val)

        for b in reads:
            add(self.lastw.get(b))
            kb = b[0] if isinstance(b, tuple) else b
            if isinstance(kb, str) and kb.startswith("ps"):
                for t in self.readers.get(b, ()):
                    if t[2] != eng:
                        add(t)
        for b in writes:
            add(self.lastw.get(b))
            for t in self.readers.get(b, ()):
                add(t)
        for k, (sem, val) in need.items():
            if self.seen[eng].get(k, 0) >= val:
                continue
            self.E[eng].wait_ge(sem, val)
            self.seen[eng][k] = val

    def op(self, eng, fn, r=(), w=(), inc=True, pw=()):
        self._waits(eng, r, tuple(w) + tuple(pw))
        ins = fn(self.E[eng])
        if not inc:
            self.pending[eng].append(tuple(r))
            return ins
        if self.cnt[eng] >= 30000:
            self.sem[eng] = self._newsem("e_" + eng)
            self.cnt[eng] = 0
        self.cnt[eng] += 1
        ins.then_inc(self.sem[eng], 1)
        tok = (self.sem[eng], self.cnt[eng], eng)
        for pr in self.pending[eng]:
            for b in pr:
                self.readers.setdefault(b, []).append(tok)
        self.pending[eng] = []
        for b in w:
            self.lastw[b] = tok
            self.readers[b] = []
        for b in r:
            self.readers.setdefault(b, []).append(tok)
        return ins

    def dma(self, eng, out, in_, r=(), w=(), key=None, **kw):
        self._waits(eng, r, w)
        if eng == "pool":
            s = self._newsem("q")
            self.all_dsems.append(s)
            self.dcnt[id(s)] = 0
            self.pool_sems.add(id(s))
            self.dsem[("poolq", self.nsem)] = s
            key = ("poolq", self.nsem)
        if key not in self.dsem:
            if self.free_dsems:
                s = self.free_dsems.pop()
            else:
                s = self._newsem("d")
                self.all_dsems.append(s)
                self.dcnt[id(s)] = 0
            self.dsem[key] = s
        s = self.dsem[key]
        self.dcnt[id(s)] += 16
        ins = self.E[eng].dma_start(out=out, in_=in_, **kw)
        ins.then_inc(s, 16)
        tok = (s, self.dcnt[id(s)], "dma")
        for b in w:
            self.lastw[b] = tok
            self.readers[b] = []
        for b in r:
            self.readers.setdefault(b, []).append(tok)
        return ins

    def barrier(self, engines=None):
        for e in self.E:
            assert not self.pending[e], "pending unsignalled reads on %s" % e
        for e in self.E:
            for e2 in self.E:
                if self.cnt[e2] == 0:
                    continue
                k = id(self.sem[e2])
                if self.seen[e].get(k, 0) < self.cnt[e2]:
                    self.E[e].wait_ge(self.sem[e2], self.cnt[e2])
                    self.seen[e][k] = self.cnt[e2]
            for s in self.all_dsems:
                v = self.dcnt[id(s)]
                if v and self.seen[e].get(id(s), 0) < v:
                    self.E[e].wait_ge(s, v)
                    self.seen[e][id(s)] = v
        self.lastw = {}
        self.readers = {}
        self.free_dsems = [x for x in self.all_dsems if id(x) not in self.pool_sems]
        self.dsem = {}


class Ctx:
    pass


class _Stop(Exception):
    pass


STOP_AT = -1
SAME_ENGINE_ORDERED = False


def build_program(last_phase=LAST_PHASE, debug=False):
    nc = bass.Bass("TRN2", target_bir_lowering=False)
    K = Ctx()
    K.nc = nc

    def din(name, shape, dt=F32):
        return nc.dram_tensor(name, list(shape), dt, kind="ExternalInput").ap()

    def dout(name, shape, dt=F32):
        return nc.dram_tensor(name, list(shape), dt, kind="ExternalOutput").ap()

    def dscr(name, shape, dt=F32):
        return nc.dram_tensor(name, list(shape), dt, kind="Internal").ap()

    I = Ctx()
    I.xin = din("xin", [NROW, D])
    I.ck = din("ck", [NSEQ, ABUF, 512])
    I.cv = din("cv", [NSEQ, ABUF, 512])
    I.spool = din("spool", [NSEQ, 15, 512])
    I.sC = din("sC", [NSEQ, 4, 512, 512])
    I.sn = din("sn", [NSEQ, 4, 512])
    I.sm = din("sm", [NSEQ, 4])
    I.sconv = din("sconv", [NSEQ, 3, 2048])
    I.norm_mix = din("norm_mix", [2, D])
    I.norm_ffn = din("norm_ffn", [2, D])
    I.norm_final = din("norm_final", [D])
    I.ab_w_in = din("ab_w_in", [D, 2048])
    I.ab_w_pool = din("ab_w_pool", [4, 128, 128])
    I.ab_pool_scale = din("ab_pool_scale", [512])
    I.ab_w_out = din("ab_w_out", [D, D])
    I.ml_w_in = din("ml_w_in", [D, 4096])
    I.ml_w_conv = din("ml_w_conv", [4, 2048])
    I.ml_b_conv = din("ml_b_conv", [2048])
    I.ml_w_q = din("ml_w_q", [512, 4, 4])
    I.ml_w_k = din("ml_w_k", [512, 4, 4])
    I.ml_w_v = din("ml_w_v", [512, 4, 4])
    I.ml_w_i = din("ml_w_i", [6144, 4])
    I.ml_b_i = din("ml_b_i", [4])
    I.ml_w_f = din("ml_w_f", [6144, 4])
    I.ml_b_f = din("ml_b_f", [4])
    I.ml_norm = din("ml_norm", [2048])
    I.ml_skip = din("ml_skip", [2048])
    I.ml_w_out = din("ml_w_out", [2048, D])
    I.ffn_w1 = din("ffn_w1", [2, D, 4096])
    I.ffn_w2 = din("ffn_w2", [2, 4096, D])
    I.c_ident = din("c_ident", [128, 128])
    I.c_mult = din("c_mult", [128, NDT + 1, 2, 128])
    I.c_eh = din("c_eh", [4, 4, 128])
    I.c_neg = din("c_neg", [128, 128])
    I.c_bd = din("c_bd", [128, 32])
    I.c_invcnt = din("c_invcnt", [128, 4, 16])

    O = Ctx()
    O.y = dout("y", [NROW, D])
    O.pak = dout("pak", [PAKR, 512])
    O.pav = dout("pav", [PAKR, 512])
    O.ppool = dout("ppool", [15, 512])
    O.pC = dout("pC", [4, 512, 512])
    O.pn = dout("pn", [4, 512])
    O.pm = dout("pm", [4])
    O.pconv = dout("pconv", [3, 2048])
    O.sak = dout("sak", [NSEQ * TS, 512])
    O.sav = dout("sav", [NSEQ * TS, 512])
    O.spool = dout("spool_o", [NSEQ, 15, 512])
    O.sC = dout("sC_o", [NSEQ, 4, 512, 512])
    O.sn = dout("sn_o", [NSEQ, 4, 512])
    O.sm = dout("sm_o", [NSEQ, 4])
    O.sconv = dout("sconv_o", [NSEQ, 3, 2048])

    hmk = dout if debug else dscr
    H1 = hmk("H1", [NROW, D])
    H2 = hmk("H2", [NROW, D])
    H3 = hmk("H3", [NROW, D])
    SG = dscr("SG", [NTI + 1, 128, 16 * 128], BF16)
    YT = dscr("YT", [NTI + 1, 128, 16 * 128], BF16)

    with contextlib.ExitStack() as gst:
        S = Sched(nc, gst)
        K.S = S
        ps = [gst.enter_context(nc.psum_tensor("ps%d" % i, [128, 512], F32)) for i in range(8)]
        K.ps = ps

        def mm(out, lhsT, rhs, start, stop, r=(), bank=None, first=False, last=False, inc=None, skip=False):
            w = [bank] if (last and bank is not None) else []
            pw = [bank] if (first and bank is not None) else []
            if inc is None:
                inc = bool(w)
            return S.op("pe", lambda e: e.matmul(out, lhsT=lhsT, rhs=rhs, start=start, stop=stop, skip_group_check=skip),
                        r=r, w=w, inc=inc, pw=pw)

        def tr(out, in_, ident, r=(), bank=None, first=False, last=False):
            w = [bank] if (last and bank is not None) else []
            pw = [bank] if (first and bank is not None) else []
            return S.op("pe", lambda e: e.transpose(out, in_, ident), r=r, w=w, inc=bool(w), pw=pw)

        def act(out, in_, func, r=(), w=(), bias=None, scale=None, accum_out=None):
            kw = {}
            if bias is not None:
                kw["bias"] = bias
            if scale is not None:
                kw["scale"] = scale
            if accum_out is not None:
                kw["accum_out"] = accum_out
            return S.op("act", lambda e: e.activation(out=out, in_=in_, func=func, **kw), r=r, w=w)

        def tt(eng, out, in0, in1, op, r=(), w=()):
            return S.op(eng, lambda e: e.tensor_tensor(out=out, in0=in0, in1=in1, op=op), r=r, w=w)

        def tsc(eng, out, in0, s1, op0, s2=None, op1=None, r=(), w=()):
            if op1 is None:
                return S.op(eng, lambda e: e.tensor_scalar(out=out, in0=in0, scalar1=s1, scalar2=None, op0=op0), r=r, w=w)
            return S.op(eng, lambda e: e.tensor_scalar(out=out, in0=in0, scalar1=s1, scalar2=s2, op0=op0, op1=op1), r=r, w=w)

        def stt(out, in0, scalar, in1, op0, op1, r=(), w=()):
            return S.op("dve", lambda e: e.scalar_tensor_tensor(out=out, in0=in0, scalar=scalar, in1=in1, op0=op0, op1=op1), r=r, w=w)

        def cp(eng, out, in_, r=(), w=()):
            if eng == "act":
                return S.op("act", lambda e: e.copy(out=out, in_=in_), r=r, w=w)
            return S.op(eng, lambda e: e.tensor_copy(out=out, in_=in_), r=r, w=w)

        def mset(eng, ap, val, w=()):
            return S.op(eng, lambda e: e.memset(ap, val), r=(), w=w)

        K.mm, K.tr, K.act, K.tt, K.tsc, K.stt, K.cp, K.mset = mm, tr, act, tt, tsc, stt, cp, mset

        def chk(n):
            if STOP_AT == n or (n == 23 and STOP_AT == 231):
                S.barrier()
                K.stopped = True
                return True
            return False

        cst = gst
        ident_f = cst.enter_context(nc.sbuf_tensor("ident_f", [128, 128], F32))
        ident_b = cst.enter_context(nc.sbuf_tensor("ident_b", [128, 128], BF16))
        S.dma("sp", ident_f[:, :], I.c_ident[:, :], w=["ident_f"], key="ident_f")
        S.dma("pool", ident_b[:, :], I.c_ident[:, :], w=["ident_b"], key="ident_b")
        K.ident_f, K.ident_b = ident_f, ident_b
        eps_rms = cst.enter_context(nc.sbuf_tensor("eps_rms", [128, 1], F32))
        eps_ln = cst.enter_context(nc.sbuf_tensor("eps_ln", [128, 1], F32))
        mset("dve", eps_rms[:, :], RMS_EPS, w=["eps_rms"])
        mset("dve", eps_ln[:, :], LN_EPS, w=["eps_ln"])

        def rsqrt_(x, n, bias_tile, scale, key):
            act(x, x, AF.Sqrt, r=[key], w=[key], bias=bias_tile[0:n, 0:1], scale=scale)
            S.op("dve", lambda e: e.reciprocal(out=x, in_=x), r=[key], w=[key])

        def norm_T(xt, gain, xn, ss, xnT_dst, keys, psT, evac="act"):
            kx, kn, ks, kT = keys["xt"], keys["xn"], keys["ss"], keys["xnT"]
            act(xn, xt, AF.Square, r=[kx], w=[kn, ks], accum_out=ss)
            rsqrt_(ss, 128, eps_rms, 1.0 / D, ks)
            stt(xn, xt, ss, gain, ALU.mult, ALU.mult, r=[kx, ks, "gain"], w=[kn])
            pT = psT[:, :].bitcast(BF16)
            for kc in range(8):
                tr(pT[:, kc * 128:(kc + 1) * 128], xn[:, kc * 128:(kc + 1) * 128], ident_b[:, :],
                   r=[kn, "ident_b"], bank="psT", first=(kc == 0), last=(kc == 7))
            cp(evac, xnT_dst, pT[:, 0:1024].rearrange("p (k t) -> p k t", k=8), r=["psT"], w=[kT])

        def phase_A():
            with contextlib.ExitStack() as st:
                def sb(name, shape, dt, stack=st):
                    return stack.enter_context(nc.sbuf_tensor("A_" + name, list(shape), dt))
                Win = sb("Win", [128, 8, 2048], BF16)
                Wout = sb("Wout", [128, 8, 1024], BF16)
                wpool = sb("wpool", [128, 4, 128], BF16)
                pscale = sb("pscale", [128, 4], F32)
                gain = sb("gain", [128, D], F32)
                mult = sb("mult", [128, NDT + 1, 2, 128], BF16)
                invc = sb("invc", [128, 4, 16], F32)
                xt = [sb("xt%d" % i, [128, D], F32) for i in range(4)]
                xn = [sb("xn%d" % i, [128, D], BF16) for i in range(4)]
                ss = [sb("ss%d" % i, [128, 1], F32) for i in range(4)]
                xnT = sb("xnT", [128, 8, 512], BF16)
                QT = sb("QT", [128, 8, 512], BF16)
                sA = sb("sA", [128, 528], F32)
                sB = sb("sB", [128, 528], F32)
                pooled = sb("pooled", [128, 512], BF16)
                PT = [sb("PT%d" % i, [128, 512], BF16) for i in range(10)]
                Osb = sb("Osb", [128, 8, 65], F32)
                rden = sb("rden", [128, 8], F32)
                atm = sb("atm", [128, 8, 64], BF16)
                hb = [sb("hb%d" % i, [128, D], F32) for i in range(4)]
                kvst = [sb("kvst%d" % i, [128, 512], F32) for i in range(2)]

                psT, psP, psS, psO = ps[0], [ps[1], ps[2]], [ps[3], ps[4]], [ps[5], ps[6]]
                cnt = {"P": 0, "S": 0, "PT": 0, "hb": 0, "kv": 0, "xt": 0, "ev": 0}

                def load_consts():
                    S.dma("pool", Win[:, :, :], I.ab_w_in.rearrange("(k p) n -> p k n", p=128), w=["Win"], key="Win")
                    S.dma("pool", Wout[:, :, :], I.ab_w_out.rearrange("(k p) n -> p k n", p=128), w=["Wout"], key="Wout")
                    S.dma("pool", wpool[:, :, :], I.ab_w_pool.rearrange("g c d -> c g d"), w=["wpool"], key="wpool")
                    S.dma("sp", pscale[:, :], I.ab_pool_scale.rearrange("(g p) -> p g", p=128), w=["pscale"], key="pscale",
                          allow_slow_non_contiguous=True)
                    S.dma("sp", gain[:, :], I.norm_mix[0].partition_broadcast(128), w=["gain"], key="gain")
                    S.dma("pool", mult[:, :, :, :], I.c_mult[:, :, :, :], w=["mult"], key="mult")
                    S.dma("sp", invc[:, :, :], I.c_invcnt[:, :, :], w=["invc"], key="invc")
                    mset("pool", QT[:, :, :], 0.0, w=["QT"])
                    mset("dve", sA[:, :], 0.0, w=["sA"])
                    mset("dve", sB[:, :], 0.0, w=["sB"])

                def nextP():
                    cnt["P"] += 1
                    return cnt["P"] % 2

                def evac_eng():
                    cnt["ev"] += 1
                    return "act" if cnt["ev"] % 2 else "dve"

                def pro_norm(rows_ap, t):
                    S.dma("sp", xt[t][:, :], rows_ap, w=[("xt", t)], key=("xt", t))
                    act(xn[t][:, :], xt[t][:, :], AF.Square, r=[("xt", t)], w=[("xn", t), ("ss", t)], accum_out=ss[t][:, :])
                    rsqrt_(ss[t][:, :], 128, eps_rms, 1.0 / D, ("ss", t))
                    stt(xn[t][:, :], xt[t][:, :], ss[t][:, :], gain[:, :], ALU.mult, ALU.mult,
                        r=[("xt", t), ("ss", t), "gain"], w=[("xn", t)])

                def pro_T(t, xnT_dst):
                    pT = psT[:, :].bitcast(BF16)
                    for kc in range(8):
                        tr(pT[:, kc * 128:(kc + 1) * 128], xn[t][:, kc * 128:(kc + 1) * 128], ident_b[:, :],
                           r=[("xn", t), "ident_b"], bank="psT", first=(kc == 0), last=(kc == 7))
                    cp("act", xnT_dst, pT[:, 0:1024].rearrange("p (k t) -> p k t", k=8), r=["psT"], w=["xnT"])

                def load_x(rows_ap, xnT_dst):
                    pro_norm(rows_ap, 0)
                    pro_T(0, xnT_dst)

                def attention(qcols_fn, NQ, klist, kt_ap, v_ap, kkey, vkey, mask_ap, qkey):
                    first, last = klist[0], klist[-1]
                    per = max(1, min(len(klist), 512 // (2 * NQ)))
                    units = [(c, klist[p0:p0 + per]) for c in range(4) for p0 in range(0, len(klist), per)]
                    AHEAD = 8
                    slots = {}

                    def s_stage(ui):
                        c, kts = units[ui]
                        n = len(kts)
                        cnt["S"] += 1
                        sbk = cnt["S"] % 2
                        Sb = psS[sbk]
                        for i, kt in enumerate(kts):
                            for h in range(2):
                                mm(Sb[:, (i * 2 + h) * NQ:(i * 2 + h + 1) * NQ],
                                   kt_ap(kt, c, h), qcols_fn(c, h), True, True,
                                   r=[kkey(kt), qkey], bank=("psS", sbk),
                                   first=(i == 0 and h == 0), last=(i == n - 1 and h == 1))
                        cnt["PT"] += 1
                        pk = cnt["PT"] % len(PT)
                        slots[ui] = pk
                        P = PT[pk]
                        act(P[:, 0:n * 2 * NQ], Sb[:, 0:n * 2 * NQ], AF.Exp, r=[("psS", sbk)], w=[("PT", pk)])
                        me = "dve"
                        Pv = P[:, 0:n * 2 * NQ].rearrange("p (a h q) -> p a h q", a=n, h=2)
                        tt(me, Pv, Pv, mask_ap(kts), ALU.mult, r=[("PT", pk), "mult"], w=[("PT", pk)])

                    def pv_stage(ui):
                        c, kts = units[ui]
                        n = len(kts)
                        pk = slots[ui]
                        P = PT[pk]
                        for i, kt in enumerate(kts):
                            for h in range(2):
                                head = 2 * c + h
                                ob = head // 4
                                fin = (kt == last and h == 1 and c in (1, 3))
                                lastpv = (i == n - 1 and h == 1)
                                mm(psO[ob][0:NQ, (head % 4) * 65:(head % 4) * 65 + 65],
                                   P[:, (i * 2 + h) * NQ:(i * 2 + h + 1) * NQ], v_ap(kt, head),
                                   (kt == first and head % 4 == 0), kt == last,
                                   r=[("PT", pk), vkey(kt)], bank=("psO", ob), first=(kt == first), last=fin,
                                   inc=(fin or lastpv), skip=True)

                    for ui in range(min(AHEAD, len(units))):
                        s_stage(ui)
                    for ui in range(len(units)):
                        pv_stage(ui)
                        if ui + AHEAD < len(units):
                            s_stage(ui + AHEAD)
                    return False

                def finish_attention(NQ, at_dst, at_key):
                    cp("act", Osb[0:NQ, 0:4, :], psO[0][0:NQ, 0:260].rearrange("p (h e) -> p h e", h=4),
                       r=[("psO", 0)], w=["Osb0"])
                    cp("dve", Osb[0:NQ, 4:8, :], psO[1][0:NQ, 0:260].rearrange("p (h e) -> p h e", h=4),
                       r=[("psO", 1)], w=["Osb1"])
                    S.op("dve", lambda e: e.reciprocal(out=rden[0:NQ, :], in_=Osb[0:NQ, :, 64]),
                         r=["Osb0", "Osb1"], w=["rden"])
                    tt("dve", atm[0:NQ, :, :], Osb[0:NQ, :, 0:64],
                       rden[0:NQ, :].unsqueeze(2).broadcast_to([NQ, 8, 64]), ALU.mult,
                       r=["Osb0", "Osb1", "rden"], w=["atm"])
                    pT = psT[:, :].bitcast(BF16)
                    af = atm[0:NQ, :, :].rearrange("p h e -> p (h e)")
                    for c in range(4):
                        tr(pT[:, c * NQ:(c + 1) * NQ], af[:, c * 128:(c + 1) * 128], ident_b[0:NQ, 0:NQ],
                           r=["atm", "ident_b"], bank="psT", first=(c == 0), last=(c == 3))
                    cp("act", at_dst, pT[:, 0:4 * NQ].rearrange("p (c q) -> p c q", c=4), r=["psT"], w=[at_key])

                def pool_mix(uT_ext, ntok, first, bt_dst, bt_key, ukey):
                    W = 16 + ntok
                    for g, wdw in enumerate((2, 4, 8, 16)):
                        u = uT_ext[:, g, :]
                        cur = u
                        curkey = ukey
                        sh = 1
                        k = 0
                        while sh < wdw:
                            dst = sA if k % 2 == 0 else sB
                            dkey = "sA" if k % 2 == 0 else "sB"
                            tt("dve", dst[:, sh:W], cur[:, sh:W], cur[:, 0:W - sh], ALU.add, r=[curkey], w=[dkey])
                            cur, curkey = dst, dkey
                            sh *= 2
                            k += 1
                        stt(pooled[:, 0:ntok], cur[:, 16:W], 1.0 / wdw, u[:, 16:W], ALU.mult, ALU.subtract,
                            r=[ukey, curkey], w=["pooled"])
                        if first:
                            other = sB if curkey == "sA" else sA
                            okey = "sB" if curkey == "sA" else "sA"
                            tt("dve", other[:, 0:16], cur[:, 16:32], invc[:, g, :], ALU.mult, r=[curkey, "invc"], w=[okey])
                            tt("dve", pooled[:, 0:16], other[:, 0:16], u[:, 16:32], ALU.subtract, r=[okey, ukey], w=["pooled"])
                        pb = nextP()
                        mm(psP[pb][:, 0:ntok], wpool[:, g, :], pooled[:, 0:ntok], True, True,
                           r=["wpool", "pooled"], bank=("psP", pb), first=True, last=True)
                        tsc("dve", bt_dst[:, g, :], psP[pb][:, 0:ntok], pscale[:, g:g + 1], ALU.mult,
                            r=[("psP", pb), "pscale"], w=[bt_key])

                def out_proj(at_fn, bt_fn, rows_src, rows_dst, rkeys, hk=None):
                    if hk is None:
                        hk = 0
                        S.dma("sp", hb[hk][:, :], rows_src, w=[("hb", hk)], key=("hb", hk))
                    for half in range(2):
                        pb = nextP()
                        for c in range(8):
                            lhs = at_fn(c) if c < 4 else bt_fn(c - 4)
                            mm(psP[pb][:, :], lhs, Wout[:, c, half * 512:(half + 1) * 512], c == 0, c == 7,
                               r=rkeys + ["Wout"], bank=("psP", pb), first=(c == 0), last=(c == 7))
                        tt("dve", hb[hk][:, half * 512:(half + 1) * 512], psP[pb][:, :],
                           hb[hk][:, half * 512:(half + 1) * 512], ALU.add,
                           r=[("psP", pb), ("hb", hk)], w=[("hb", hk)])
                    S.dma("sp", rows_dst, hb[hk][:, :], r=[("hb", hk)], w=[], key=("hb", hk))

                def proj_tm(xcols_fn, col0, nrow):
                    pb = nextP()
                    for kc in range(8):
                        mm(psP[pb][0:nrow, :], xcols_fn(kc), Win[:, kc, col0:col0 + 512], kc == 0, kc == 7,
                           r=["xnT", "Win"], bank=("psP", pb), first=(kc == 0), last=(kc == 7))
                    return pb

                def to_out(pb, nrow, dst_ap, row0=0):
                    cnt["kv"] += 1
                    kk = cnt["kv"] % 2
                    cp("dve", kvst[kk][0:128, :], psP[pb][0:128, :], r=[("psP", pb)], w=[("kvst", kk)])
                    if STOP_AT == 231:
                        return
                    S.dma("sp", dst_ap, kvst[kk][row0:row0 + nrow, :], r=[("kvst", kk)], w=[], key=("kvst", kk))

                load_consts()
                if chk(1):
                    return
                with contextlib.ExitStack() as st1:
                    KT = sb("KT", [128, 4, RT * 128], BF16, st1)
                    V = sb("V", [128, RT, 8, 65], BF16, st1)
                    uT = sb("uT", [128, 4, 16 + 512], F32, st1)
                    AT = sb("AT", [128, 4, 512], BF16, st1)
                    BT = sb("BT", [128, 4, 512], BF16, st1)
                    mset("pool", V[:, :, :, :], 1.0, w=[("V", i) for i in range(RT)])
                    mset("dve", uT[:, :, 0:16], 0.0, w=["uT"])
                    for t in range(4):
                        pro_norm(I.xin[t * 128:(t + 1) * 128, :], t)
                    for g in range(NTI // 4):
                        for t in range(4):
                            pro_T(t, xnT[:, :, t * 128:(t + 1) * 128])
                        for t in range(4):
                            ti = 4 * g + t
                            S.dma("sp", hb[t][:, :], I.xin[ti * 128:(ti + 1) * 128, :], w=[("hb", t)], key=("hb", t))
                        if chk(2):
                            return
                        for oc in range(16):
                            if 8 <= oc < 12:
                                continue
                            pb = nextP()
                            for kc in range(8):
                                mm(psP[pb][:, :], Win[:, kc, oc * 128:(oc + 1) * 128], xnT[:, kc, :], kc == 0, kc == 7,
                                   r=["xnT", "Win"], bank=("psP", pb), first=(kc == 0), last=(kc == 7))
                            if oc < 4:
                                act(QT[0:64, 2 * oc, :], psP[pb][0:64, :], AF.Copy, r=[("psP", pb)], w=["QT"], scale=0.125)
                                act(QT[64:128, 2 * oc + 1, :], psP[pb][64:128, :], AF.Copy, r=[("psP", pb)], w=["QT"], scale=0.125)
                            elif oc < 8:
                                sl = (4 * g) % RT
                                cp(evac_eng(), KT[:, oc - 4, sl * 128:(sl + 4) * 128], psP[pb][:, :], r=[("psP", pb)],
                                   w=[("K", (4 * g + i) % RT) for i in range(4)])
                            else:
                                cp(evac_eng(), uT[:, oc - 12, 16:528], psP[pb][:, :], r=[("psP", pb)], w=["uT"])
                        if chk(21):
                            return
                        for t in range(4):
                            ti = 4 * g + t
                            sl = ti % RT
                            want_out = ti >= NTI - PAKR // 128
                            r0 = (ti - (NTI - PAKR // 128)) * 128
                            pb = proj_tm(lambda kc: xnT[:, kc, t * 128:(t + 1) * 128], 1024, 128)
                            cp("act", V[:, sl, :, 0:64], psP[pb][:, :].rearrange("p (h e) -> p h e", h=8),
                               r=[("psP", pb)], w=[("V", sl)])
                            if chk(22):
                                return
                            if want_out:
                                to_out(pb, 128, O.pav[r0:r0 + 128, :])
                                if chk(23):
                                    return
                                pb = proj_tm(lambda kc: xnT[:, kc, t * 128:(t + 1) * 128], 512, 128)
                                to_out(pb, 128, O.pak[r0:r0 + 128, :])
                            if ti == NTI - 1:
                                pb = proj_tm(lambda kc: xnT[:, kc, t * 128:(t + 1) * 128], 1536, 128)
                                to_out(pb, 15, O.ppool[:, :], row0=113)
                        if chk(3):
                            return
                        for t in range(4):
                            qt = 4 * g + t
                            klist = list(range(max(0, qt - 16), qt + 1))
                            if attention(
                                lambda c, h: QT[:, 2 * c + h, t * 128:(t + 1) * 128], 128, klist,
                                lambda kt, c, h: KT[:, c, (kt % RT) * 128:(kt % RT + 1) * 128],
                                lambda kt, head: V[:, kt % RT, head, :],
                                lambda kt: ("K", kt % RT), lambda kt: ("V", kt % RT),
                                lambda kts: mult[:, kts[0] - (qt - 16):kts[0] - (qt - 16) + len(kts), :, :], "QT"):
                                return
                            if chk(35):
                                return
                            finish_attention(128, AT[:, :, t * 128:(t + 1) * 128], "AT")
                            if t == 0 and g + 1 < NTI // 4:
                                for t2 in range(4):
                                    ti2 = 4 * (g + 1) + t2
                                    pro_norm(I.xin[ti2 * 128:(ti2 + 1) * 128, :], t2)
                            if chk(36):
                                return
                        if chk(4):
                            return
                        pool_mix(uT, 512, g == 0, BT, "BT", "uT")
                        if chk(5):
                            return
                        cp("dve", uT[:, :, 0:16], uT[:, :, 512:528], r=["uT"], w=["uT"])
                        for t in range(4):
                            ti = 4 * g + t
                            out_proj(lambda c: AT[:, c, t * 128:(t + 1) * 128], lambda c: BT[:, c, t * 128:(t + 1) * 128],
                                     I.xin[ti * 128:(ti + 1) * 128, :], H1[ti * 128:(ti + 1) * 128, :], ["AT", "BT"], hk=t)
                    S.barrier()
                if chk(6):
                    return
                with contextlib.ExitStack() as st2:
                    uTs = sb("uTs", [128, 4, 128], F32, st2)
                    stg = sb("stg", [128, 16, 512], BF16, st2)
                    KTs = sb("KTs", [128, 4, 17 * 128], BF16, st2)
                    Vs = sb("Vs", [128, 17, 8, 65], BF16, st2)
                    ATs = sb("ATs", [128, 4, 128], BF16, st2)
                    BTs = sb("BTs", [128, 4, 128], BF16, st2)
                    pbuf = sb("pbuf", [15, 512], F32, st2)
                    uxs = sb("uxs", [128, 4, 16 + TS], F32, st2)
                    xnTs = xnT[:, :, 0:128]
                    load_x(I.xin[NT:NT + 128, :], xnTs)
                    QTs = QT[:, :, 0:128]
                    kTn = sb("kTn", [128, 4, 128], BF16, st2)
                    for oc in range(16):
                        if 8 <= oc < 12:
                            continue
                        pb = nextP()
                        for kc in range(8):
                            mm(psP[pb][:, 0:128], Win[:, kc, oc * 128:(oc + 1) * 128], xnTs[:, kc, :], kc == 0, kc == 7,
                               r=["xnT", "Win"], bank=("psP", pb), first=(kc == 0), last=(kc == 7))
                        if oc < 4:
                            act(QTs[0:64, 2 * oc, :], psP[pb][0:64, 0:128], AF.Copy, r=[("psP", pb)], w=["QT"], scale=0.125)
                            act(QTs[64:128, 2 * oc + 1, :], psP[pb][64:128, 0:128], AF.Copy, r=[("psP", pb)], w=["QT"], scale=0.125)
                        elif oc < 8:
                            cp("dve", kTn[:, oc - 4, :], psP[pb][:, 0:128], r=[("psP", pb)], w=["kTn"])
                        else:
                            cp("dve", uTs[:, oc - 12, :], psP[pb][:, 0:128], r=[("psP", pb)], w=["uTs"])
                    pb = proj_tm(lambda kc: xnTs[:, kc, :], 512, 128)
                    to_out(pb, NSEQ * TS, O.sak[:, :])
                    pb = proj_tm(lambda kc: xnTs[:, kc, :], 1024, 128)
                    to_out(pb, NSEQ * TS, O.sav[:, :])
                    mset("pool", Vs[:, :, :, :], 1.0, w=["Vs", "Vs16"])
                    mset("pool", Vs[:, 16, :, 0:64], 0.0, w=["Vs16"])
                    mset("dve", KTs[:, :, 16 * 128:17 * 128], 0.0, w=["KTs16"])
                    mset("dve", ATs[:, :, :], 0.0, w=["ATs"])
                    mset("dve", BTs[:, :, :], 0.0, w=["BTs"])
                    mset("dve", uxs[:, :, :], 0.0, w=["uxs"])
                    pT = psT[:, :].bitcast(BF16)
                    for n in range(NSEQ):
                        c0 = n * TS
                        S.dma("pool", stg[:, :, :], I.ck[n].rearrange("(t p) f -> p t f", p=128), w=["stg"], key="stg")
                        for t2 in range(0, 16, 2):
                            for a in range(2):
                                for c in range(4):
                                    tr(pT[:, (a * 4 + c) * 128:(a * 4 + c + 1) * 128],
                                       stg[:, t2 + a, c * 128:(c + 1) * 128], ident_b[:, :],
                                       r=["stg", "ident_b"], bank="psT", first=(a == 0 and c == 0), last=(a == 1 and c == 3))
                            cp(evac_eng(), KTs[:, :, t2 * 128:(t2 + 2) * 128].rearrange("p c (a k) -> p c a k", a=2),
                               pT[:, 0:1024].rearrange("p (a c k) -> p c a k", a=2, c=4), r=["psT"], w=["KTs"])
                        S.dma("pool", stg[:, :, :], I.cv[n].rearrange("(t p) f -> p t f", p=128), w=["stg"], key="stg")
                        cp("pool", Vs[:, 0:16, :, 0:64], stg[:, :, :].rearrange("p t (h e) -> p t h e", h=8), r=["stg"], w=["Vs"])
                        cp("dve", KTs[:, :, 16 * 128:16 * 128 + TS], kTn[:, :, c0:c0 + TS], r=["kTn"], w=["KTs16"])
                        pb = proj_tm(lambda kc: xnTs[:, kc, c0:c0 + TS], 1024, TS)
                        cp("act", Vs[0:TS, 16, :, 0:64], psP[pb][0:TS, :].rearrange("p (h e) -> p h e", h=8),
                           r=[("psP", pb)], w=["Vs16"])
                        attention(
                            lambda c, h: QTs[:, 2 * c + h, c0:c0 + TS], TS, list(range(17)),
                            lambda kt, c, h: KTs[:, c, kt * 128:(kt + 1) * 128],
                            lambda kt, head: Vs[:, kt, head, :],
                            lambda kt: ("KTs16" if kt == 16 else "KTs"), lambda kt: ("Vs16" if kt == 16 else "Vs"),
                            lambda kts: mult[:, kts[0]:kts[0] + len(kts), :, 0:TS], "QT")
                        finish_attention(TS, ATs[:, :, c0:c0 + TS], "ATs")
                        S.dma("sp", pbuf[:, :], I.spool[n], w=["pbuf"], key="pbuf")
                        pf = ps[7]
                        for gq in range(4):
                            tr(pf[:, gq * 16:gq * 16 + 15], pbuf[:, gq * 128:(gq + 1) * 128], ident_f[0:15, 0:15],
                               r=["pbuf", "ident_f"], bank="ps7", first=(gq == 0), last=(gq == 3))
                        cp("dve", uxs[:, :, 1:16], pf[:, 0:64].rearrange("p (g j) -> p g j", g=4)[:, :, 0:15],
                           r=["ps7"], w=["uxs"])
                        cp("dve", uxs[:, :, 16:16 + TS], uTs[:, :, c0:c0 + TS], r=["uTs"], w=["uxs"])
                        pool_mix(uxs, TS, False, BTs[:, :, c0:c0 + TS], "BTs", "uxs")
                        S.dma("sp", O.spool[n, 0:15 - TS, :], I.spool[n, TS:15, :], w=[], key=("spc", n))
                        pb = proj_tm(lambda kc: xnTs[:, kc, c0:c0 + TS], 1536, TS)
                        to_out(pb, TS, O.spool[n, 15 - TS:15, :])
                    out_proj(lambda c: ATs[:, c, :], lambda c: BTs[:, c, :], I.xin[NT:NT + 128, :], H1[NT:NT + 128, :],
                             ["ATs", "BTs"])
                    S.barrier()

        def phase_FFN(layer, Hin, Hout, final):
            GT = 2
            with contextlib.ExitStack() as st:
                def sb(name, shape, dt):
                    return st.enter_context(nc.sbuf_tensor("F%d_" % layer + name, list(shape), dt))
                W1 = sb("W1", [128, 8, 4096], BF16)
                W2 = sb("W2", [128, 32, 1024], BF16)
                gain = sb("gain", [128, D], F32)
                gfin = sb("gfin", [128, D], F32) if final else None
                xt = [sb("xt%d" % i, [128, D], F32) for i in range(2 * GT)]
                xn = [sb("xn%d" % i, [128, D], BF16) for i in range(GT)]
                ss = [sb("ss%d" % i, [128, 1], F32) for i in range(GT)]
                xnT = [sb("xnT%d" % i, [128, 8, 128 * GT], BF16) for i in range(2)]
                hT = sb("hT", [128, 32, 128 * GT], BF16)
                rl = [sb("rl%d" % i, [128, 128 * GT], BF16) for i in range(2)]
                yo = [sb("yo%d" % i, [128, D], F32) for i in range(2)] if final else None
                ssf = sb("ssf", [128, 1], F32)
                S.dma("pool", W1[:, :, :], I.ffn_w1[layer].rearrange("(k p) n -> p k n", p=128), w=["W1"], key="W1")
                S.dma("pool", W2[:, :, :], I.ffn_w2[layer].rearrange("(f p) n -> p f n", p=128), w=["W2"], key="W2")
                S.dma("sp", gain[:, :], I.norm_ffn[layer].partition_broadcast(128), w=["gain"], key="gain")
                if final:
                    S.dma("sp", gfin[:, :], I.norm_final.partition_broadcast(128), w=["gfin"], key="gfin")
                psT, psP, psY = ps[0], [ps[1], ps[2], ps[3]], [ps[4], ps[5], ps[6], ps[7]]
                cnt = {"P": 0, "Y": 0, "rl": 0, "yo": 0}
                groups = [list(range(GT * g, GT * g + GT)) for g in range(NTI // GT)] + [[NTI]]

                def pro_norm(gi):
                    for t, ti in enumerate(groups[gi]):
                        slot = (gi % 2) * GT + t
                        S.dma("sp", xt[slot][:, :], Hin[ti * 128:(ti + 1) * 128, :], w=[("xt", slot)], key=("xt", slot))
                        act(xn[t][:, :], xt[slot][:, :], AF.Square, r=[("xt", slot)], w=[("xn", t), ("ss", t)], accum_out=ss[t][:, :])
                        rsqrt_(ss[t][:, :], 128, eps_rms, 1.0 / D, ("ss", t))
                        stt(xn[t][:, :], xt[slot][:, :], ss[t][:, :], gain[:, :], ALU.mult, ALU.mult,
                            r=[("xt", slot), ("ss", t), "gain"], w=[("xn", t)])

                def pro_T(gi):
                    xb = xnT[gi % 2]
                    pT = psT[:, :].bitcast(BF16)
                    for t, ti in enumerate(groups[gi]):
                        for kc in range(8):
                            tr(pT[:, kc * 128:(kc + 1) * 128], xn[t][:, kc * 128:(kc + 1) * 128], ident_b[:, :],
                               r=[("xn", t), "ident_b"], bank="psT", first=(kc == 0), last=(kc == 7))
                        cp("act", xb[:, :, t * 128:(t + 1) * 128], pT[:, 0:1024].rearrange("p (k t) -> p k t", k=8),
                           r=["psT"], w=["xnTg%d" % (gi % 2)])

                pro_norm(0)
                pro_T(0)
                for gi, tiles in enumerate(groups):
                    ntok = 128 * len(tiles)
                    xb = xnT[gi % 2]
                    xkey = "xnTg%d" % (gi % 2)
                    for f in range(32):
                        cnt["P"] += 1
                        pb = cnt["P"] % 3
                        for kc in range(8):
                            mm(psP[pb][:, 0:ntok], W1[:, kc, f * 128:(f + 1) * 128], xb[:, kc, 0:ntok], kc == 0, kc == 7,
                               r=[xkey, "W1"], bank=("psP", pb), first=(kc == 0), last=(kc == 7))
                        cnt["rl"] += 1
                        rk = cnt["rl"] % 2
                        act(rl[rk][:, 0:ntok], psP[pb][:, 0:ntok], AF.Relu, r=[("psP", pb)], w=[("rl", rk)])
                        tt("pool" if f % 2 else "dve", hT[:, f, 0:ntok], rl[rk][:, 0:ntok], rl[rk][:, 0:ntok], ALU.mult,
                           r=[("rl", rk)], w=["hT"])
                    if gi + 1 < len(groups):
                        pro_norm(gi + 1)
                    for t, ti in enumerate(tiles):
                        slot = (gi % 2) * GT + t
                        for half in range(2):
                            cnt["Y"] += 1
                            yb = cnt["Y"] % 4
                            for f in range(32):
                                mm(psY[yb][:, :], hT[:, f, t * 128:(t + 1) * 128], W2[:, f, half * 512:(half + 1) * 512],
                                   f == 0, f == 31, r=["hT", "W2"], bank=("psY", yb), first=(f == 0), last=(f == 31))
                            tt("dve", xt[slot][:, half * 512:(half + 1) * 512], psY[yb][:, :],
                               xt[slot][:, half * 512:(half + 1) * 512], ALU.add,
                               r=[("psY", yb), ("xt", slot)], w=[("xt", slot)])
                        if not final:
                            S.dma("sp", Hout[ti * 128:(ti + 1) * 128, :], xt[slot][:, :], r=[("xt", slot)], w=[],
                                  key=("xt", slot))
                        else:
                            cnt["yo"] += 1
                            yk = cnt["yo"] % 2
                            act(yo[yk][:, :], xt[slot][:, :], AF.Square, r=[("xt", slot)], w=[("yo", yk), "ssf"], accum_out=ssf[:, :])
                            rsqrt_(ssf[:, :], 128, eps_rms, 1.0 / D, "ssf")
                            stt(yo[yk][:, :], xt[slot][:, :], ssf[:, :], gfin[:, :], ALU.mult, ALU.mult,
                                r=[("xt", slot), "ssf", "gfin"], w=[("yo", yk)])
                            S.dma("sp", Hout[ti * 128:(ti + 1) * 128, :], yo[yk][:, :], r=[("yo", yk)], w=[],
                                  key=("yo", yk))
                    if gi + 1 < len(groups):
                        pro_T(gi + 1)
                S.barrier()

        def phase_B0(Hin):
            with contextlib.ExitStack() as st:
                def sb(name, shape, dt):
                    return st.enter_context(nc.sbuf_tensor("B0_" + name, list(shape), dt))
                Wg = sb("Wg", [128, 8, 2048], BF16)
                gain = sb("gain", [128, D], F32)
                xt = [sb("xt%d" % i, [128, D], F32) for i in range(2)]
                xn = [sb("xn%d" % i, [128, D], BF16) for i in range(4)]
                ss = [sb("ss%d" % i, [128, 1], F32) for i in range(4)]
                xnT = [sb("xnT%d" % i, [128, 8, 512], BF16) for i in range(2)]
                sg = [sb("sg%d" % i, [128, 4, 16, 128], BF16) for i in range(2)]
                S.dma("pool", Wg[:, :, :], I.ml_w_in[:, 2048:4096].rearrange("(k p) n -> p k n", p=128), w=["Wg"], key="Wg")
                S.dma("sp", gain[:, :], I.norm_mix[1].partition_broadcast(128), w=["gain"], key="gain")
                psT, psP = ps[0], [ps[1], ps[2], ps[3]]
                cnt = {"P": 0, "xt": 0}
                groups = [list(range(4 * g, 4 * g + 4)) for g in range(NTI // 4)] + [[NTI]]

                def pro_norm(gi):
                    for t, ti in enumerate(groups[gi]):
                        cnt["xt"] += 1
                        xk = cnt["xt"] % 2
                        S.dma("sp", xt[xk][:, :], Hin[ti * 128:(ti + 1) * 128, :], w=[("xt", xk)], key=("xt", xk))
                        act(xn[t][:, :], xt[xk][:, :], AF.Square, r=[("xt", xk)], w=[("xn", t), ("ss", t)], accum_out=ss[t][:, :])
                        rsqrt_(ss[t][:, :], 128, eps_rms, 1.0 / D, ("ss", t))
                        stt(xn[t][:, :], xt[xk][:, :], ss[t][:, :], gain[:, :], ALU.mult, ALU.mult,
                            r=[("xt", xk), ("ss", t), "gain"], w=[("xn", t)])

                def pro_T(gi):
                    xb = xnT[gi % 2]
                    pT = psT[:, :].bitcast(BF16)
                    for t, ti in enumerate(groups[gi]):
                        for kc in range(8):
                            tr(pT[:, kc * 128:(kc + 1) * 128], xn[t][:, kc * 128:(kc + 1) * 128], ident_b[:, :],
                               r=[("xn", t), "ident_b"], bank="psT", first=(kc == 0), last=(kc == 7))
                        cp("dve", xb[:, :, t * 128:(t + 1) * 128], pT[:, 0:1024].rearrange("p (k t) -> p k t", k=8),
                           r=["psT"], w=["xnTg%d" % (gi % 2)])

                pro_norm(0)
                pro_T(0)
                for gi, tiles in enumerate(groups):
                    ntok = 128 * len(tiles)
                    xb = xnT[gi % 2]
                    xkey = "xnTg%d" % (gi % 2)
                    sgb = sg[gi % 2]
                    skey = "sg%d" % (gi % 2)
                    for oc in range(16):
                        if oc == 4 and gi + 1 < len(groups):
                            pro_norm(gi + 1)
                        cnt["P"] += 1
                        pb = cnt["P"] % 3
                        for kc in range(8):
                            mm(psP[pb][:, 0:ntok], Wg[:, kc, oc * 128:(oc + 1) * 128], xb[:, kc, 0:ntok], kc == 0, kc == 7,
                               r=[xkey, "Wg"], bank=("psP", pb), first=(kc == 0), last=(kc == 7))
                        act(sgb[:, 0:len(tiles), oc, :], psP[pb][:, 0:ntok].rearrange("p (t k) -> p t k", k=128),
                            AF.Sigmoid, r=[("psP", pb)], w=[skey])
                    if gi + 1 < len(groups):
                        pro_T(gi + 1)
                    for t, ti in enumerate(tiles):
                        S.dma("sp", SG[ti], sgb[:, t, :, :].rearrange("p c k -> p (c k)"), r=[skey], w=[], key=skey)
                S.barrier()

        def phase_B1(Hin):
            with contextlib.ExitStack() as st:
                def sb(name, shape, dt, stack=st):
                    return stack.enter_context(nc.sbuf_tensor("B1_" + name, list(shape), dt))
                Wx = sb("Wx", [128, 8, 2048], BF16)
                BD = sb("BD", [128, 3, 16, 128], BF16)
                wif = sb("wif", [128, 48, 8], BF16)
                wcv = sb("wcv", [128, 16, 4], F32)
                bcv = sb("bcv", [128, 16], F32)
                gnm = sb("gnm", [128, 16], F32)
                skp = sb("skp", [128, 16], F32)
                gain = sb("gain", [128, D], F32)
                Eh = sb("Eh", [4, 4, 128], F32)
                NEG = sb("NEG", [128, 128], BF16)
                ones4 = sb("ones4", [4, 128], F32)
                onesb = sb("onesb", [128, 1], BF16)
                bi = sb("bi", [4, 1], F32)
                nbf_ = sb("nbf_", [4, 1], F32)
                one41 = sb("one41", [4, 1], F32)
                lnsc = sb("lnsc", [128, 1], F32)
                C = sb("C", [128, 4, 4, 512], F32)
                Cb = sb("Cb", [128, 4, 4, 512], BF16)
                nst = sb("nst", [128, 4, 4], F32)
                nstb = sb("nstb", [128, 4, 4], BF16)
                mcar = sb("mcar", [4, 1], F32)
                xt = [sb("xt%d" % i, [128, D], F32) for i in range(2)]
                xn = sb("xn", [128, D], BF16)
                ss = sb("ss", [128, 1], F32)
                xnT = sb("xnT", [128, 8, 128], BF16)
                xmT = sb("xmT", [128, 16, 3 + 128], F32)
                xmb = [sb("xmb%d" % i, [128, 16, 128], BF16) for i in range(2)]
                cacc = [sb("cacc%d" % i, [128, 128], F32) for i in range(4)]
                caT = [sb("caT%d" % i, [128, 16, 128], BF16) for i in range(2)]
                qT = [sb("qT%d" % i, [128, 16, 128], BF16) for i in range(2)]
                kT = [sb("kT%d" % i, [128, 16, 128], BF16) for i in range(2)]
                vT = sb("vT", [128, 16, 128], BF16)
                ktm = sb("ktm", [128, 2048], BF16)
                vtm = sb("vtm", [128, 2048], BF16)
                sgt = sb("sgt", [128, 4, 128], BF16)
                gpre = [sb("gpre%d" % i, [4, 256], F32) for i in range(2)]
                gts = [{k: sb("g%d_" % i + k, [4, 128], F32) for k in ("ip", "ef", "sp", "ft", "mt", "nF", "u", "w", "si", "em")}
                       for i in range(2)]
                gzero = sb("gzero", [4, 128], F32)
                dg = sb("dg", [4, 4], F32)
                cols = [sb("cols%d" % i, [128, 8], F32) for i in range(2)]
                scB = [sb("scB%d" % i, [128, 4], F32) for i in range(2)]
                Dp4 = sb("Dp4", [128, 4, 128], F32)
                aT4 = sb("aT4", [128, 4, 128], BF16)
                qs = sb("qs", [128, 16, 128], BF16)
                sib = sb("sib", [128, 4, 128], BF16)
                bst4 = sb("bst4", [128, 4, 6], F32)
                mv4 = sb("mv4", [128, 4, 2], F32)
                den4 = sb("den4", [128, 4], F32)
                rd4 = sb("rd4", [128, 4], F32)
                t4a = sb("t4a", [128, 4], F32)
                t4b = sb("t4b", [128, 4], F32)
                A4 = sb("A4", [128, 4], F32)
                B4 = sb("B4", [128, 4], F32)
                hn = [sb("hn%d" % i, [128, 512], BF16) for i in range(2)]
                hnT = sb("hnT", [128, 16, 128], BF16)
                wk = [sb("wk%d" % i, [128, 512], BF16) for i in range(2)]
                y1 = sb("y1", [128, 4, 128], F32)
                y2 = sb("y2", [128, 4, 128], F32)
                yT = sb("yT", [128, 16, 128], BF16)
                xe = sb("xe", [128, 16, 3 + TS], F32)

                S.dma("pool", Wx[:, :, :], I.ml_w_in[:, 0:2048].rearrange("(k p) n -> p k n", p=128), w=["Wx"], key="Wx")
                S.dma("pool", wif[:, :, 0:4], I.ml_w_i.rearrange("(c p) h -> p c h", p=128), w=["wif0"], key="wif")
                S.dma("pool", wif[:, :, 4:8], I.ml_w_f.rearrange("(c p) h -> p c h", p=128), w=["wif1"], key="wif")
                for j in range(4):
                    S.dma("sp", wcv[:, :, j], I.ml_w_conv[j].rearrange("(c p) -> p c", p=128), w=["wcv"], key="wcv",
                          allow_slow_non_contiguous=True)
                for dst, src, kk in ((bcv, I.ml_b_conv, "bcv"), (gnm, I.ml_norm, "gnm"), (skp, I.ml_skip, "skp")):
                    S.dma("sp", dst[:, :], src.rearrange("(c p) -> p c", p=128), w=[kk], key=kk, allow_slow_non_contiguous=True)
                S.dma("sp", gain[:, :], I.norm_mix[1].partition_broadcast(128), w=["gain"], key="gain")
                S.dma("sp", Eh[:, :, :], I.c_eh.rearrange("h k s -> k h s"), w=["Eh"], key="Eh")
                S.dma("pool", NEG[:, :], I.c_neg[:, :], w=["NEG"], key="NEG")
                S.dma("sp", bi[:, :], I.ml_b_i.rearrange("(h o) -> h o", o=1), w=["bi"], key="bi")
                S.dma("sp", nbf_[:, :], I.ml_b_f.rearrange("(h o) -> h o", o=1), w=["nbf"], key="nbf")
                tsc("dve", nbf_[:, :], nbf_[:, :], -1.0, ALU.mult, r=["nbf"], w=["nbf"])
                mset("dve", ones4[:, :], 1.0, w=["ones4"])
                mset("dve", onesb[:, :], 1.0, w=["onesb"])
                mset("dve", one41[:, :], 1.0, w=["one41"])
                mset("dve", lnsc[:, :], math.log(512.0 ** -0.5), w=["lnsc"])
                mset("dve", gzero[:, :], 0.0, w=["g_zero"])
                with contextlib.ExitStack() as stw:
                    Wexp = sb("Wexp", [128, 3, 16, 4], F32, stw)
                    bdm = sb("bdm", [128, 32], F32, stw)
                    for j, wsrc in enumerate((I.ml_w_q, I.ml_w_k, I.ml_w_v)):
                        S.dma("sp", Wexp[:, j, :, :], wsrc.rearrange("(c g) i o -> (g i) c o", g=32), w=["Wexp"], key="Wexp")
                    S.dma("sp", bdm[:, :], I.c_bd[:, :], w=["bdm"], key="bdm")
                    for j in range(3):
                        for c in range(16):
                            tt("dve", BD[:, j, c, :].rearrange("p (g o) -> p g o", o=4),
                               Wexp[:, j, c, :].unsqueeze(1).broadcast_to([128, 32, 4]),
                               bdm[:, :].unsqueeze(2).broadcast_to([128, 32, 4]), ALU.mult,
                               r=["Wexp", "bdm"], w=["BD"])
                    S.barrier()

                psT, psF = ps[0], ps[1]
                psA, psS, psN0, psN1, psD = ps[2], ps[3], ps[4], ps[5], ps[6]
                FB = [(ps[0], "psT"), (ps[1], "psF"), (ps[7], "psF2")]
                cnt = {"xt": 0, "F": 0, "K": 0}

                def nextF():
                    cnt["F"] += 1
                    return FB[cnt["F"] % 3]

                def load_x(rows_ap):
                    cnt["xt"] += 1
                    xk = cnt["xt"] % 2
                    S.dma("sp", xt[xk][:, :], rows_ap, w=[("xt", xk)], key=("xt", xk))
                    norm_T(xt[xk][:, :], gain[:, :], xn[:, :], ss[:, :], xnT[:, :, :],
                           {"xt": ("xt", xk), "xn": "xn", "ss": "ss", "xnT": "xnT"}, psT)

                def proj_xm(p):
                    for c4 in range(4):
                        fb, fk = nextF()
                        for cc in range(4):
                            c = c4 * 4 + cc
                            for kc in range(8):
                                mm(fb[:, cc * 128:(cc + 1) * 128], Wx[:, kc, c * 128:(c + 1) * 128], xnT[:, kc, :],
                                   kc == 0, kc == 7, r=["xnT", "Wx"], bank=fk,
                                   first=(kc == 0 and cc == 0), last=(kc == 7 and cc == 3))
                        cp("act", xmT[:, c4 * 4:(c4 + 1) * 4, 3:131], fb[:, :].rearrange("p (c k) -> p c k", c=4),
                           r=[fk], w=["xmT"])
                        cp("dve", xmb[p][:, c4 * 4:(c4 + 1) * 4, :], fb[:, :].rearrange("p (c k) -> p c k", c=4),
                           r=[fk], w=[("xmb", p)])
                        yield

                def conv_silu(ext_fn, ncol, ca_dst_fn, rkey, cakey):
                    for c8 in range(0, 16, 4):
                        for j in range(4):
                            for cc in range(4):
                                c = c8 + cc
                                e = ext_fn(c)
                                if j == 0:
                                    tsc("dve", cacc[cc][:, 0:ncol], e[:, 0:ncol], wcv[:, c, 0:1], ALU.mult, r=[rkey, "wcv"],
                                        w=[("cacc", cc)])
                                else:
                                    stt(cacc[cc][:, 0:ncol], e[:, j:j + ncol], wcv[:, c, j:j + 1], cacc[cc][:, 0:ncol],
                                        ALU.mult, ALU.add, r=[rkey, "wcv", ("cacc", cc)], w=[("cacc", cc)])
                            yield
                        for cc in range(4):
                            c = c8 + cc
                            act(ca_dst_fn(c), cacc[cc][:, 0:ncol], AF.Silu, r=[("cacc", cc), "bcv"], w=[cakey],
                                bias=bcv[:, c:c + 1])
                        yield

                def qkv_gates(p):
                    for j, (src, dst, key, skey) in enumerate(((caT[p], qT[p], ("qT", p), ("caT", p)),
                                                               (caT[p], kT[p], ("kT", p), ("caT", p)),
                                                               (xmb[p], vT, "vT", ("xmb", p)))):
                        for c4 in range(4):
                            fb, fk = nextF()
                            for cc in range(4):
                                c = c4 * 4 + cc
                                mm(fb[:, cc * 128:(cc + 1) * 128], BD[:, j, c, :], src[:, c, :], True, True,
                                   r=["BD", skey], bank=fk, first=(cc == 0), last=(cc == 3))
                            cp("act" if c4 % 2 else "dve", dst[:, c4 * 4:(c4 + 1) * 4, :],
                               fb[:, :].rearrange("p (c k) -> p c k", c=4), r=[fk], w=[key])
                            yield
                    for gsel in range(2):
                        for ch in range(48):
                            srcT = (qT[p], kT[p], vT)[ch // 16]
                            mm(psF[0:4, gsel * 128:(gsel + 1) * 128], wif[:, ch, gsel * 4:gsel * 4 + 4], srcT[:, ch % 16, :],
                               ch == 0, ch == 47, r=["wif0", "wif1", ("qT", p), ("kT", p), "vT"], bank="psF",
                               first=(ch == 0 and gsel == 0), last=(ch == 47 and gsel == 1))
                    cp("dve", gpre[p][:, :], psF[0:4, 0:256], r=["psF"], w=[("gpre", p)])
                    yield

                def gate_math(L, c0, p):
                    g = gts[p]
                    G = lambda k: ("g_" + k, p)
                    sl = slice(0, L)
                    tsc("dve", g["ip"][:, sl], gpre[p][:, c0:c0 + L], bi[:, 0:1], ALU.add, r=[("gpre", p), "bi"], w=[G("ip")])
                    act(g["ef"][:, sl], gpre[p][:, 128 + c0:128 + c0 + L], AF.Exp, r=[("gpre", p), "nbf"], w=[G("ef")],
                        bias=nbf_[:, 0:1], scale=-1.0)
                    yield
                    act(g["sp"][:, sl], g["ef"][:, sl], AF.Ln, r=[G("ef"), "one41"], w=[G("sp")], bias=one41[:, 0:1])
                    yield
                    tsc("dve", g["ft"][:, sl], g["sp"][:, sl], -1.0, ALU.mult, r=[G("sp")], w=[G("ft")])
                    S.op("dve", lambda e: e.tensor_tensor_scan(out=g["nF"][:, sl], data0=g["sp"][:, sl], data1=gzero[:, sl],
                                                               initial=0.0, op0=ALU.add, op1=ALU.add),
                         r=[G("sp"), "g_zero"], w=[G("nF")])
                    yield
                    S.op("dve", lambda e: e.tensor_tensor_scan(out=g["mt"][:, sl], data0=g["ft"][:, sl], data1=g["ip"][:, sl],
                                                               initial=mcar[:, 0:1], op0=ALU.add, op1=ALU.max),
                         r=[G("ft"), G("ip"), "mcar"], w=[G("mt")])
                    tt("dve", g["u"][:, sl], g["ip"][:, sl], g["nF"][:, sl], ALU.add, r=[G("ip"), G("nF")], w=[G("u")])
                    yield
                    stt(g["w"][:, sl], g["nF"][:, sl], -1.0, g["mt"][:, sl], ALU.mult, ALU.subtract, r=[G("nF"), G("mt")],
                        w=[G("w")])
                    act(g["em"][:, sl], g["mt"][:, sl], AF.Exp, r=[G("mt")], w=[G("em")], scale=-2.0)
                    yield
                    act(g["si"][:, sl], g["w"][:, sl], AF.Exp, r=[G("w"), "mcar"], w=[G("si")], bias=mcar[:, 0:1])
                    yield
                    cp("dve", mcar[:, :], g["mt"][:, L - 1:L], r=[G("mt")], w=["mcar"])
                    tr(psF[0:L, 0:4], g["si"][:, sl], ident_f[0:4, 0:4], r=[G("si"), "ident_f"], bank="psF", first=True)
                    tr(psF[0:L, 4:8], g["em"][:, sl], ident_f[0:4, 0:4], r=[G("em"), "ident_f"], bank="psF", last=True)
                    cp("dve", cols[p][0:L, :], psF[0:L, 0:8], r=["psF"], w=[("cols", p)])
                    tsc("dve", dg[:, :], ident_f[0:4, 0:4], g["si"][:, L - 1:L], ALU.mult, r=["ident_f", G("si")], w=["dg"])
                    yield
                    mm(psF[:, 8:12], ones4[:, :], dg[:, :], True, True, r=["ones4", "dg"], bank="psF", first=True, last=True)
                    cp("dve", scB[p][:, :], psF[:, 8:12], r=["psF"], w=[("scB", p)])
                    yield

                def cell(L, c0, p, hn_dst_fn):
                    g = gts[p]
                    G = lambda k: ("g_" + k, p)
                    NB = [psN0, psN1, psA, psS]
                    NBk = ["psN0", "psN1", "psA", "psS"]
                    qTp, kTp, caTp, xmbp, colsp, scBp = qT[p], kT[p], caT[p], xmb[p], cols[p], scB[p]
                    for j, (src, dstm, key, skey) in enumerate(((caTp, ktm, "ktm", ("caT", p)), (xmbp, vtm, "vtm", ("xmb", p)))):
                        for c4 in range(4):
                            kb, kbk = NB[c4], NBk[c4]
                            for cc in range(4):
                                c = c4 * 4 + cc
                                mm(kb[0:L, cc * 128:(cc + 1) * 128], src[:, c, c0:c0 + L], BD[:, 1 + j, c, :], True, True,
                                   r=["BD", skey], bank=kbk, first=(cc == 0), last=(cc == 3))
                            cp("act" if c4 % 2 else "dve", dstm[0:L, c4 * 512:(c4 + 1) * 512], kb[0:L, :],
                               r=[kbk], w=[key])
                            yield
                    for h in range(4):
                        mm(psA[0:L, h * 128:h * 128 + L], g["u"][:, 0:L], Eh[:, h, 0:L], True, False, r=[G("u"), "Eh"],
                           bank="psA", first=(h == 0))
                        mm(psA[0:L, h * 128:h * 128 + L], Eh[:, h, 0:L], g["w"][:, 0:L], False, False, r=[G("w"), "Eh"])
                        mm(psA[0:L, h * 128:h * 128 + L], ident_b[0:L, 0:L], NEG[0:L, 0:L], False, True, r=["ident_b", "NEG"],
                           bank="psA", last=(h == 3), inc=True)
                    yield
                    for h in range(4):
                        for dc in range(4):
                            mm(psS[0:L, h * 128:h * 128 + L], kTp[:, 4 * h + dc, c0:c0 + L], qTp[:, 4 * h + dc, c0:c0 + L],
                               dc == 0, dc == 3, r=[("kT", p), ("qT", p)], bank="psS", first=(h == 0 and dc == 0),
                               last=(h == 3 and dc == 3), inc=(dc == 3))
                    yield
                    A4v = psA[0:L, :].rearrange("p (h s) -> p h s", h=4)[:, :, 0:L]
                    G4v = psS[0:L, :].rearrange("p (h s) -> p h s", h=4)[:, :, 0:L]
                    act(Dp4[0:L, :, 0:L], A4v, AF.Exp, r=["psA", "lnsc"], w=["Dp4"], bias=lnsc[0:L, 0:1])
                    yield
                    for h in range(4):
                        mm(psA[:, h * 128:h * 128 + L], Eh[:, h, :], g["si"][:, 0:L], True, True, r=["Eh", G("si")],
                           bank="psA", first=(h == 0), last=(h == 3), inc=(h == 3))
                    yield
                    tt("dve", aT4[0:L, :, 0:L], Dp4[0:L, :, 0:L], G4v, ALU.mult, r=["Dp4", "psS"], w=["aT4"])
                    yield
                    sbv = psA[:, :].rearrange("p (h s) -> p h s", h=4)[:, :, 0:L]
                    cp("act", sib[:, :, 0:L], sbv, r=["psA"], w=["sib"])
                    yield
                    tt("dve", qs[:, :, 0:L].rearrange("p (h d) s -> p h d s", h=4),
                       qTp[:, :, c0:c0 + L].rearrange("p (h d) s -> p h d s", h=4),
                       sib[:, :, 0:L].unsqueeze(2).broadcast_to([128, 4, 4, L]), ALU.mult, r=[("qT", p), "sib"], w=["qs"])
                    yield
                    for h in range(4):
                        for dc in range(4):
                            mm(NB[h][0:L, :], qs[:, 4 * h + dc, 0:L], Cb[:, h, dc, :], dc == 0, False,
                               r=["qs", ("Cb", h)], bank=NBk[h], first=(dc == 0))
                        mm(NB[h][0:L, :], aT4[0:L, h, 0:L], vtm[0:L, h * 512:(h + 1) * 512], False, True, r=["aT4", "vtm"],
                           bank=NBk[h], last=True)
                        for dc in range(4):
                            mm(psD[0:L, 16 + h:17 + h], qs[:, 4 * h + dc, 0:L], nstb[:, h, dc:dc + 1], dc == 0, False,
                               r=["qs", ("nstb", h)], bank="psD", first=(h == 0 and dc == 0))
                        mm(psD[0:L, 16 + h:17 + h], aT4[0:L, h, 0:L], onesb[0:L, :], False, True, r=["aT4", "onesb"],
                           bank="psD", last=(h == 3), inc=True)
                        yield
                    for h in range(4):
                        S.op("dve", lambda e: e.bn_stats(out=bst4[0:L, h, :], in_=NB[h][0:L, :]), r=[NBk[h]], w=["bst4"])
                        S.op("dve", lambda e: e.bn_aggr(out=mv4[0:L, h, :], in_=bst4[0:L, h, :]), r=["bst4"], w=["mv4"])
                        yield
                    cp("dve", den4[0:L, :], psD[0:L, 16:20], r=["psD"], w=["den4"])
                    yield
                    tt("dve", den4[0:L, :], den4[0:L, :], den4[0:L, :], ALU.mult, r=["den4"], w=["den4"])
                    yield
                    tt("dve", den4[0:L, :], den4[0:L, :], colsp[0:L, 4:8], ALU.max, r=["den4", ("cols", p)], w=["den4"])
                    yield
                    stt(t4b[0:L, :], den4[0:L, :], LN_EPS, mv4[0:L, :, 1], ALU.mult, ALU.add, r=["den4", "mv4"], w=["t4b"])
                    yield
                    act(t4b[0:L, :], t4b[0:L, :], AF.Sqrt, r=["t4b"], w=["t4b"])
                    yield
                    S.op("dve", lambda e: e.reciprocal(out=A4[0:L, :], in_=t4b[0:L, :]), r=["t4b"], w=["A4"])
                    yield
                    stt(B4[0:L, :], mv4[0:L, :, 0], -1.0, A4[0:L, :], ALU.mult, ALU.mult, r=["mv4", "A4"], w=["B4"])
                    yield
                    for h in range(4):
                        pT = NB[h][:, :].bitcast(BF16)
                        hk = h % 2
                        if h % 2 == 0:
                            act(hn[hk][0:L, :], NB[h][0:L, :], AF.Identity, r=[NBk[h], "A4", "B4"], w=[("hn", hk)],
                                bias=B4[0:L, h:h + 1], scale=A4[0:L, h:h + 1])
                        else:
                            tsc("dve", hn[hk][0:L, :], NB[h][0:L, :], A4[0:L, h:h + 1], ALU.mult, B4[0:L, h:h + 1], ALU.add,
                                r=[NBk[h], "A4", "B4"], w=[("hn", hk)])
                        yield
                        for dc in range(4):
                            tr(pT[:, dc * L:(dc + 1) * L], hn[hk][0:L, dc * 128:(dc + 1) * 128], ident_b[0:L, 0:L],
                               r=[("hn", hk), "ident_b"], bank=NBk[h], first=(dc == 0), last=(dc == 3))
                        cp("act" if h % 2 else "dve", hn_dst_fn(h), pT[:, 0:4 * L].rearrange("p (c q) -> p c q", c=4),
                           r=[NBk[h]], w=["hnT"])
                        yield
                    for h in range(4):
                        wkk = h % 2
                        if h % 2 == 0:
                            tsc("dve", wk[wkk][0:L, :], ktm[0:L, h * 512:(h + 1) * 512], Dp4[0:L, h, L - 1:L], ALU.mult,
                                r=["ktm", "Dp4"], w=[("wk", wkk)])
                        else:
                            act(wk[wkk][0:L, :], ktm[0:L, h * 512:(h + 1) * 512], AF.Copy, r=["ktm", "Dp4"], w=[("wk", wkk)],
                                scale=Dp4[0:L, h, L - 1:L])
                        yield
                        for dc in range(4):
                            mm(psD[:, 24 + dc:25 + dc], wk[wkk][0:L, dc * 128:(dc + 1) * 128], onesb[0:L, :], True, True,
                               r=[("wk", wkk), "onesb"], bank="psD", first=(dc == 0), last=(dc == 3))
                        stt(nst[:, h, :], nst[:, h, :], scBp[:, h:h + 1], psD[:, 24:28], ALU.mult, ALU.add,
                            r=["psD", ("scB", p), ("nst", h)], w=[("nst", h)])
                        cp("pool", nstb[:, h, :], nst[:, h, :], r=[("nst", h)], w=[("nstb", h)])
                        yield
                        for dc in range(4):
                            bi_ = (4 * h + dc) % 4
                            bk, bkey = NB[bi_], NBk[bi_]
                            mm(bk[:, :], wk[wkk][0:L, dc * 128:(dc + 1) * 128], vtm[0:L, h * 512:(h + 1) * 512], True, True,
                               r=[("wk", wkk), "vtm"], bank=bkey, first=True, last=True)
                            stt(C[:, h, dc, :], C[:, h, dc, :], scBp[:, h:h + 1], bk[:, :], ALU.mult, ALU.add,
                                r=[bkey, ("scB", p), ("C", (h, dc))], w=[("C", (h, dc))])
                            cp("act", Cb[:, h, dc, :], C[:, h, dc, :], r=[("C", (h, dc))], w=[("Cb", h)])
                            yield

                def finish_tile(ti, p):
                    for c4 in range(4):
                        cs = slice(c4 * 4, c4 * 4 + 4)
                        S.dma("sp", sgt[:, :, :].rearrange("p c k -> p (c k)"), SG[ti, :, c4 * 512:(c4 + 1) * 512], w=["sgt"], key="sgt")
                        tt("dve", y1[:, :, :], hnT[:, cs, :], gnm[:, cs].unsqueeze(2).broadcast_to([128, 4, 128]), ALU.mult,
                           r=["hnT", "gnm"], w=["y1"])
                        tt("pool", y2[:, :, :], caT[p][:, cs, :], skp[:, cs].unsqueeze(2).broadcast_to([128, 4, 128]), ALU.mult,
                           r=[("caT", p), "skp"], w=["y2"])
                        yield
                        tt("dve", y1[:, :, :], y1[:, :, :], y2[:, :, :], ALU.add, r=["y1", "y2"], w=["y1"])
                        yield
                        tt("dve", yT[:, cs, :], y1[:, :, :], sgt[:, :, :], ALU.mult, r=["y1", "sgt"], w=["yT"])
                        yield
                    S.dma("sp", YT[ti], yT[:, :, :].rearrange("p c k -> p (c k)"), r=["yT"], w=[], key="yT")
                    yield

                def drain(gen):
                    for _ in gen:
                        pass

                def interleave(gens):
                    gens = [g_ for g_ in gens if g_ is not None]
                    while gens:
                        for g_ in list(gens):
                            try:
                                next(g_)
                            except StopIteration:
                                gens.remove(g_)

                def FE(ti):
                    p = ti % 2
                    load_x(Hin[ti * 128:(ti + 1) * 128, :])
                    yield
                    yield from proj_xm(p)
                    yield from conv_silu(lambda c: xmT[:, c, :], 128, lambda c: caT[p][:, c, :], "xmT", ("caT", p))
                    if ti == NTI - 1:
                        for j in range(3):
                            S.dma("sp", O.pconv[j].rearrange("(c p) -> p c", p=128), xmT[:, :, 128 + j], r=["xmT"], w=[],
                                  key="xmTo", allow_slow_non_contiguous=True)
                    cp("dve", xmT[:, :, 0:3], xmT[:, :, 128:131], r=["xmT"], w=["xmT"])
                    yield
                    yield from qkv_gates(p)
                    yield from gate_math(128, 0, p)

                def BE(ti):
                    p = ti % 2
                    yield from cell(128, 0, p, lambda h: hnT[:, 4 * h:4 * h + 4, :])
                    yield from finish_tile(ti, p)

                mset("dve", C[:, :, :, :], 0.0, w=[("C", (h, dc)) for h in range(4) for dc in range(4)])
                mset("pool", Cb[:, :, :, :], 0.0, w=[("Cb", h) for h in range(4)])
                mset("dve", nst[:, :, :], 0.0, w=[("nst", h) for h in range(4)])
                mset("dve", nstb[:, :, :], 0.0, w=[("nstb", h) for h in range(4)])
                mset("dve", mcar[:, :], 0.0, w=["mcar"])
                mset("dve", xmT[:, :, 0:3], 0.0, w=["xmT"])
                drain(FE(0))
                for ti in range(NTI):
                    interleave([BE(ti), FE(ti + 1) if ti + 1 < NTI else None])
                for h in range(4):
                    S.dma("sp", O.pC[h].rearrange("(dc p) e -> p dc e", p=128), C[:, h, :, :], r=[("C", (h, dc)) for dc in range(4)], w=[], key=("Co", h))
                for h in range(4):
                    S.dma("sp", O.pn[h].rearrange("(dc p) -> p dc", p=128), nst[:, h, :], r=[("nst", h)], w=[],
                          key="nsto", allow_slow_non_contiguous=True)
                S.dma("sp", O.pm.rearrange("(h o) -> h o", o=1), mcar[:, :], r=["mcar"], w=[], key="mcaro")

                load_x(Hin[NT:NT + 128, :])
                drain(proj_xm(0))
                mset("dve", caT[0][:, :, :], 0.0, w=[("caT", 0)])
                mset("dve", hnT[:, :, :], 0.0, w=["hnT"])
                for n in range(NSEQ):
                    c0 = n * TS
                    for j in range(3):
                        S.dma("sp", xe[:, :, j], I.sconv[n, j].rearrange("(c p) -> p c", p=128), w=["xe"], key="xe",
                              allow_slow_non_contiguous=True)
                    cp("dve", xe[:, :, 3:3 + TS], xmT[:, :, 3 + c0:3 + c0 + TS], r=["xmT"], w=["xe"])
                    drain(conv_silu(lambda c: xe[:, c, :], TS, lambda c: caT[0][:, c, c0:c0 + TS], "xe", ("caT", 0)))
                    for j in range(3):
                        S.dma("sp", O.sconv[n, j].rearrange("(c p) -> p c", p=128), xe[:, :, TS + j], r=["xe"], w=[],
                              key="xe", allow_slow_non_contiguous=True)
                drain(qkv_gates(0))
                for n in range(NSEQ):
                    c0 = n * TS
                    for h in range(4):
                        S.dma("sp", C[:, h, :, :], I.sC[n, h].rearrange("(dc p) e -> p dc e", p=128), w=[("C", (h, dc)) for dc in range(4)], key=("Co", h))
                        cp("act", Cb[:, h, :, :], C[:, h, :, :], r=[("C", (h, dc)) for dc in range(4)], w=[("Cb", h)])
                    for h in range(4):
                        S.dma("sp", nst[:, h, :], I.sn[n, h].rearrange("(dc p) -> p dc", p=128), w=[("nst", hh) for hh in range(4)],
                              key="nsto", allow_slow_non_contiguous=True)
                    for h in range(4):
                        cp("pool", nstb[:, h, :], nst[:, h, :], r=[("nst", h)], w=[("nstb", h)])
                    S.dma("sp", mcar[:, :], I.sm[n].rearrange("(h o) -> h o", o=1), w=["mcar"], key="mcaro")
                    drain(gate_math(TS, c0, 0))
                    drain(cell(TS, c0, 0, lambda h: hnT[:, 4 * h:4 * h + 4, c0:c0 + TS]))
                    for h in range(4):
                        S.dma("sp", O.sC[n, h].rearrange("(dc p) e -> p dc e", p=128), C[:, h, :, :], r=[("C", (h, dc)) for dc in range(4)], w=[],
                              key=("Co", h))
                    for h in range(4):
                        S.dma("sp", O.sn[n, h].rearrange("(dc p) -> p dc", p=128), nst[:, h, :], r=[("nst", h)],
                              w=[], key="nsto", allow_slow_non_contiguous=True)
                    S.dma("sp", O.sm[n].rearrange("(h o) -> h o", o=1), mcar[:, :], r=["mcar"], w=[], key="mcaro")
                drain(finish_tile(NTI, 0))
                S.barrier()

        def phase_B2(Hin, Hout):
            with contextlib.ExitStack() as st:
                def sb(name, shape, dt):
                    return st.enter_context(nc.sbuf_tensor("B2_" + name, list(shape), dt))
                Wo = sb("Wo", [128, 16, 1024], BF16)
                yt = [sb("yt%d" % i, [128, 16, 128], BF16) for i in range(3)]
                xr = [sb("xr%d" % i, [128, D], F32) for i in range(3)]
                S.dma("pool", Wo[:, :, :], I.ml_w_out.rearrange("(c p) n -> p c n", p=128), w=["Wo"], key="Wo")
                psY = [ps[i] for i in range(4)]
                cy = 0

                def loads(ti):
                    k = ti % 3
                    S.dma("sp", yt[k][:, :, :].rearrange("p c k -> p (c k)"), YT[ti], w=[("yt", k)], key=("yt", k))
                    S.dma("sp", xr[k][:, :], Hin[ti * 128:(ti + 1) * 128, :], w=[("xr", k)], key=("xr", k))

                loads(0)
                loads(1)
                for ti in range(NTI + 1):
                    k = ti % 3
                    for half in range(2):
                        cy += 1
                        yb = cy % 4
                        for c in range(16):
                            mm(psY[yb][:, :], yt[k][:, c, :], Wo[:, c, half * 512:(half + 1) * 512], c == 0, c == 15,
                               r=[("yt", k), "Wo"], bank=("psY", yb), first=(c == 0), last=(c == 15))
                        tt("dve", xr[k][:, half * 512:(half + 1) * 512], psY[yb][:, :], xr[k][:, half * 512:(half + 1) * 512],
                           ALU.add, r=[("psY", yb), ("xr", k)], w=[("xr", k)])
                    if ti + 2 < NTI + 1:
                        loads(ti + 2)
                    S.dma("sp", Hout[ti * 128:(ti + 1) * 128, :], xr[k][:, :], r=[("xr", k)], w=[], key=("xr", k))
                S.barrier()

        K.stopped = False
        S.barrier()
        if not chk(0):
            phase_A()
        if last_phase >= 1 and not K.stopped:
            phase_FFN(0, H1, H2, False)
        if last_phase >= 2 and not K.stopped:
            phase_B0(H2)
            if not chk(40):
                phase_B1(H2)
                if not chk(41):
                    phase_B2(H2, H3)
        if last_phase >= 3 and not K.stopped:
            phase_FFN(1, H3, O.y, True)
        print("semaphores used:", S.nsem)
    return nc


def _consts():
    ident = np.eye(128, dtype=np.float32)
    mult = np.zeros((128, NDT + 1, 2, 128), np.float32)
    k = np.arange(128)[:, None]
    q = np.arange(128)[None, :]
    for j in range(NDT):
        delta = 16 - j
        dist = 128 * delta + q - k
        m = np.zeros((128, 128), np.float32)
        for win, d in ((128, 1), (512, 4), (2048, 16)):
            m += ((dist >= 0) & (dist % d == 0) & (dist // d <= 128)).astype(np.float32)
        mult[:, j, 0, :] = m
        mult[:, j, 1, :] = m
    eh = np.zeros((4, 4, 128), np.float32)
    for h in range(4):
        eh[h, h, :] = 1.0
    neg = np.where(k <= q, 0.0, NEGBIG).astype(np.float32)
    bd = (np.arange(128)[:, None] // 4 == np.arange(32)[None, :]).astype(np.float32)
    invc = np.zeros((128, 4, 16), np.float32)
    for g, w in enumerate((2, 4, 8, 16)):
        invc[:, g, :] = 1.0 / np.minimum(np.arange(16) + 1.0, float(w))
    return dict(c_ident=ident, c_mult=mult, c_eh=eh, c_neg=neg, c_bd=bd, c_invcnt=invc)


_NC_CACHE = {}


def kernel(**inp):
    f = lambda a: np.ascontiguousarray(np.asarray(a, dtype=np.float32))
    x_prompt = f(inp["x_prompt"])
    x_sample = f(inp["x_sample"])
    consts = _consts()
    shared = dict(
        norm_mix=f(inp["norm_mix"]), norm_ffn=f(inp["norm_ffn"]), norm_final=f(inp["norm_final"]),
        ab_w_in=f(inp["ab_w_in"])[0], ab_w_pool=f(inp["ab_w_pool"])[0], ab_pool_scale=f(inp["ab_pool_scale"])[0],
        ab_w_out=f(inp["ab_w_out"])[0], ml_w_in=f(inp["ml_w_in"])[0], ml_w_conv=f(inp["ml_w_conv"])[0],
        ml_b_conv=f(inp["ml_b_conv"])[0], ml_w_q=f(inp["ml_w_q"])[0], ml_w_k=f(inp["ml_w_k"])[0],
        ml_w_v=f(inp["ml_w_v"])[0], ml_w_i=f(inp["ml_w_i"])[0], ml_b_i=f(inp["ml_b_i"])[0],
        ml_w_f=f(inp["ml_w_f"])[0], ml_b_f=f(inp["ml_b_f"])[0], ml_norm=f(inp["ml_norm"])[0],
        ml_skip=f(inp["ml_skip"])[0], ml_w_out=f(inp["ml_w_out"])[0], ffn_w1=f(inp["ffn_w1"]), ffn_w2=f(inp["ffn_w2"]),
    )
    shared.update(consts)
    ck = f(inp["cache_a_k"])[0].reshape(32, ABUF, 512)
    cv = f(inp["cache_a_v"])[0].reshape(32, ABUF, 512)
    spool = f(inp["state_pool"])[0]
    sC = f(inp["state_ml_C"])[0]
    sn = f(inp["state_ml_n"])[0]
    sm = f(inp["state_ml_m"])[0]
    sconv = f(inp["state_ml_conv"])[0]
    in_maps = []
    for c in range(8):
        b = c % 4
        xin = np.zeros((NROW, D), np.float32)
        xin[:NT] = x_prompt[b]
        xin[NT:NT + NSEQ * TS] = x_sample[4 * c:4 * c + 4].reshape(NSEQ * TS, D)
        m = dict(shared)
        m.update(xin=xin, ck=np.ascontiguousarray(ck[4 * c:4 * c + 4]), cv=np.ascontiguousarray(cv[4 * c:4 * c + 4]),
                 spool=np.ascontiguousarray(spool[4 * c:4 * c + 4]), sC=np.ascontiguousarray(sC[4 * c:4 * c + 4]),
                 sn=np.ascontiguousarray(sn[4 * c:4 * c + 4]), sm=np.ascontiguousarray(sm[4 * c:4 * c + 4]),
                 sconv=np.ascontiguousarray(sconv[4 * c:4 * c + 4]))
        in_maps.append(m)
    if "nc" not in _NC_CACHE:
        _NC_CACHE["nc"] = build_program()
    nc = _NC_CACHE["nc"]
    res = run_bass_kernel_spmd(nc, in_maps, core_ids=list(range(8)))
    R = res.results
    y_prompt = np.stack([R[b]["y"][:NT] for b in range(4)])
    y_sample = np.concatenate([R[c]["y"][NT:NT + 16].reshape(4, 4, D) for c in range(8)])
    p_ak = np.stack([R[b]["pak"].reshape(ABUF, 8, 64) for b in range(4)])[None]
    p_av = np.stack([R[b]["pav"].reshape(ABUF, 8, 64) for b in range(4)])[None]
    p_pool = np.stack([R[b]["ppool"] for b in range(4)])[None]
    p_C = np.stack([R[b]["pC"] for b in range(4)])[None]
    p_n = np.stack([R[b]["pn"] for b in range(4)])[None]
    p_m = np.stack([R[b]["pm"] for b in range(4)])[None]
    p_conv = np.stack([R[b]["pconv"] for b in range(4)])[None]
    s_ak = np.concatenate([R[c]["sak"].reshape(4, 4, 8, 64) for c in range(8)])[None]
    s_av = np.concatenate([R[c]["sav"].reshape(4, 4, 8, 64) for c in range(8)])[None]
    s_pool = np.concatenate([R[c]["spool_o"] for c in range(8)])[None]
    s_C = np.concatenate([R[c]["sC_o"] for c in range(8)])[None]
    s_n = np.concatenate([R[c]["sn_o"] for c in range(8)])[None]
    s_m = np.concatenate([R[c]["sm_o"] for c in range(8)])[None]
    s_conv = np.concatenate([R[c]["sconv_o"] for c in range(8)])[None]
    outs = (y_prompt, y_sample, p_ak, p_av, p_pool, p_C, p_n, p_m, p_conv, s_ak, s_av, s_pool, s_C, s_n, s_m, s_conv)
    return tuple(np.ascontiguousarray(o, dtype=np.float32) for o in outs)
```
